# Optimizing a Trainium2 kernel written in Bass

```python
import jax, jax.numpy as jnp
from jax import lax
import numpy as np

D_MODEL = 1024
BATCH = 8
SEQ = 2048
DEPTH = 2
DEC_BATCH = 32
DEC_SEQ = 4
PAST_LEN = 8192
PAGE_SIZE = 128

EXPAND = 2
D_MIX = EXPAND * D_MODEL
W_CONV = D_MIX // 4
W_POOL = D_MIX // 4
W_SB = D_MIX // 4
W_MEM = D_MIX - W_CONV - W_POOL - W_SB
CONV_K = 31
POOL_WINDOWS = (2, 4, 8, 16)
N_POOL_GROUPS = len(POOL_WINDOWS)
POOL_GROUP = W_POOL // N_POOL_GROUPS
POOL_HIST = max(POOL_WINDOWS) - 1
SB_HEADS = 8
SB_HEAD_DIM = W_SB // SB_HEADS
SB_BIAS_INIT = -8.0
MEM_HEADS = 4
MEM_HEAD_DIM = W_MEM // MEM_HEADS
N_MEM = 256
Q_BLOCK = 128
EPS = 1e-6
SPLIT_SIZES = (W_CONV, W_CONV, W_CONV, W_POOL, W_POOL, W_SB, W_SB, W_SB, W_SB, W_MEM, W_MEM)
D_IN = sum(SPLIT_SIZES)

kernel_name = "hybrid_conv_pool_stickbreak_memory_decoder_step"


def rms_norm(x, g):
    xf = x.astype(jnp.float32)
    y = xf * lax.rsqrt(jnp.mean(xf * xf, axis=-1, keepdims=True) + EPS)
    return (y * g.astype(jnp.float32)).astype(x.dtype)


def layer_norm(x, g, b):
    xf = x.astype(jnp.float32)
    mu = jnp.mean(xf, axis=-1, keepdims=True)
    var = jnp.mean(jnp.square(xf - mu), axis=-1, keepdims=True)
    y = (xf - mu) * lax.rsqrt(var + EPS) * g.astype(jnp.float32) + b.astype(jnp.float32)
    return y.astype(x.dtype)


def conv_branch(a, b, hist, w_dw, b_dw, ln_g, ln_b, w_pw, b_pw):
    u = a * jax.nn.sigmoid(b)
    u_ext = jnp.concatenate([hist.astype(u.dtype), u], axis=1)
    y = lax.conv_general_dilated(u_ext, w_dw[:, None, :].astype(u.dtype), (1,), 'VALID',
                                 dimension_numbers=('NWC', 'WIO', 'NWC'),
                                 feature_group_count=W_CONV) + b_dw
    y = jax.nn.silu(layer_norm(y, ln_g, ln_b))
    y = y @ w_pw + b_pw
    return y, u_ext[:, -(CONV_K - 1):]


def pool_branch(u, hist, pos, w_pool, scale):
    n, l, _ = u.shape
    u_ext = jnp.concatenate([hist.astype(u.dtype), u], axis=1)
    c = jnp.cumsum(u_ext.astype(jnp.float32), axis=1)
    c = jnp.pad(c, ((0, 0), (1, 0), (0, 0)))
    end = c[:, POOL_HIST + 1:]
    means = []
    for gi, w in enumerate(POOL_WINDOWS):
        sl = slice(gi * POOL_GROUP, (gi + 1) * POOL_GROUP)
        start = c[:, POOL_HIST + 1 - w: POOL_HIST + 1 - w + l, sl]
        cnt = jnp.minimum(w, pos + 1).astype(jnp.float32)[None, :, None]
        means.append((end[..., sl] - start) / cnt)
    mean = jnp.concatenate(means, axis=-1).astype(u.dtype)
    d = (mean - u).reshape(n, l, N_POOL_GROUPS, POOL_GROUP)
    y = jnp.einsum('nlgc,gcd->nlgd', d, w_pool).reshape(n, l, W_POOL) * scale
    return y, u_ext[:, -POOL_HIST:]


def sb_block(q, k, v, bias, q_pos, k_pos):
    z = (jnp.einsum('nqhd,nkhd->nhqk', q, k).astype(jnp.float32) * (SB_HEAD_DIM ** -0.5)
         + bias.astype(jnp.float32)[None, :, None, None])
    mask = (k_pos[None, :] < q_pos[:, None])[None, None]
    log_beta = jax.nn.log_sigmoid(z)
    log_1m = jnp.where(mask, jax.nn.log_sigmoid(-z), 0.0)
    rc = lax.cumsum(log_1m, axis=3, reverse=True)
    a = jnp.exp(jnp.where(mask, log_beta + rc - log_1m, -jnp.inf))
    return jnp.einsum('nhqk,nkhd->nqhd', a.astype(v.dtype), v)


def sb_prompt(q, k, v, bias):
    n, l, h, dh = q.shape
    nb = l // Q_BLOCK
    qb = q.reshape(n, nb, Q_BLOCK, h, dh).transpose(1, 0, 2, 3, 4)
    pos = jnp.arange(l)
    qpos = pos.reshape(nb, Q_BLOCK)
    out = lax.map(lambda args: sb_block(args[0], k, v, bias, args[1], pos), (qb, qpos))
    return out.transpose(1, 0, 2, 3, 4).reshape(n, l, h, dh)


def mem_attend(q, mk, mv):
    s = jnp.einsum('nqhd,nmhd->nhqm', q, mk).astype(jnp.float32) * (MEM_HEAD_DIM ** -0.5)
    p = jax.nn.softmax(s, axis=-1)
    return jnp.einsum('nhqm,nmhd->nqhd', p.astype(mv.dtype), mv)


def mem_kv(mem, g, w):
    n = mem.shape[0]
    kv = rms_norm(mem, g) @ w
    mk, mv = jnp.split(kv, 2, axis=-1)
    return (mk.reshape(n, -1, MEM_HEADS, MEM_HEAD_DIM), mv.reshape(n, -1, MEM_HEADS, MEM_HEAD_DIM))


def mixer_layer(x, pos, conv_hist, pool_hist, sb_fn, mk, mv, pre_g, post_g, w_in,
                w_dw, b_dw, ln_g, ln_b, w_pw, b_pw, pool_w, pool_scale, sb_bias, w_out):
    n, l, _ = x.shape
    h = rms_norm(x, pre_g)
    z = h @ w_in
    ca, cb, cg, pu, pg, q, k, v, sg, mq, mg = jnp.split(
        z, np.cumsum(SPLIT_SIZES)[:-1].tolist(), axis=-1)
    y_c, conv_new = conv_branch(ca, cb, conv_hist, w_dw, b_dw, ln_g, ln_b, w_pw, b_pw)
    y_p, pool_new = pool_branch(pu, pool_hist, pos, pool_w, pool_scale)
    q = q.reshape(n, l, SB_HEADS, SB_HEAD_DIM)
    k = k.reshape(n, l, SB_HEADS, SB_HEAD_DIM)
    v = v.reshape(n, l, SB_HEADS, SB_HEAD_DIM)
    y_s = sb_fn(q, k, v, sb_bias).reshape(n, l, W_SB)
    y_m = mem_attend(mq.reshape(n, l, MEM_HEADS, MEM_HEAD_DIM), mk, mv).reshape(n, l, W_MEM)
    mix = jnp.concatenate([y_c * jax.nn.silu(cg), y_p * jax.nn.silu(pg),
                           y_s * jax.nn.silu(sg), y_m * jax.nn.silu(mg)], axis=-1)
    out = rms_norm(mix @ w_out, post_g)
    return x + out, conv_new, pool_new, k, v


def setup_inputs(seed: int = 0) -> dict:
    key = jax.random.key(seed)
    ks = jax.random.split(key, 32)
    f32 = jnp.float32
    n_pages = PAST_LEN // PAGE_SIZE
    n_used = DEC_BATCH * n_pages
    n_phys = (5 * n_used) // 4

    def nrm(k, shape, s=1.0):
        return s * jax.random.normal(k, shape, f32)

    page_table = jax.random.permutation(ks[0], n_phys)[:n_used].reshape(DEC_BATCH, n_pages).astype(jnp.int32)
    return {
        "x_prompt": nrm(ks[1], (BATCH, SEQ, D_MODEL)),
        "x_sample": nrm(ks[2], (DEC_BATCH, DEC_SEQ, D_MODEL)),
        "mem_prompt": nrm(ks[3], (BATCH, N_MEM, D_MODEL)),
        "cache_k": nrm(ks[4], (DEPTH, n_phys, PAGE_SIZE, SB_HEADS, SB_HEAD_DIM)),
        "cache_v": nrm(ks[5], (DEPTH, n_phys, PAGE_SIZE, SB_HEADS, SB_HEAD_DIM)),
        "cache_mem_k": nrm(ks[6], (DEPTH, DEC_BATCH, N_MEM, MEM_HEADS, MEM_HEAD_DIM)),
        "cache_mem_v": nrm(ks[7], (DEPTH, DEC_BATCH, N_MEM, MEM_HEADS, MEM_HEAD_DIM)),
        "state_conv": nrm(ks[8], (DEPTH, DEC_BATCH, CONV_K - 1, W_CONV), 0.5),
        "state_pool": nrm(ks[9], (DEPTH, DEC_BATCH, POOL_HIST, W_POOL)),
        "page_table": page_table,
        "pre_g": 1.0 + nrm(ks[10], (DEPTH, D_MODEL), 0.05),
        "post_g": 1.0 + nrm(ks[11], (DEPTH, D_MODEL), 0.05),
        "w_in": nrm(ks[12], (DEPTH, D_MODEL, D_IN), D_MODEL ** -0.5),
        "conv_w_dw": nrm(ks[13], (DEPTH, CONV_K, W_CONV), CONV_K ** -0.5),
        "conv_b_dw": nrm(ks[14], (DEPTH, W_CONV), 0.02),
        "conv_ln_g": 1.0 + nrm(ks[15], (DEPTH, W_CONV), 0.05),
        "conv_ln_b": nrm(ks[16], (DEPTH, W_CONV), 0.02),
        "conv_w_pw": nrm(ks[17], (DEPTH, W_CONV, W_CONV), W_CONV ** -0.5),
        "conv_b_pw": nrm(ks[18], (DEPTH, W_CONV), 0.02),
        "pool_w": nrm(ks[19], (DEPTH, N_POOL_GROUPS, POOL_GROUP, POOL_GROUP), POOL_GROUP ** -0.5),
        "pool_scale": 1.0 + nrm(ks[20], (DEPTH, W_POOL), 0.05),
        "sb_bias": SB_BIAS_INIT + nrm(ks[25], (DEPTH, SB_HEADS), 0.1),
        "mem_g": 1.0 + nrm(ks[21], (DEPTH, D_MODEL), 0.05),
        "w_mem_kv": nrm(ks[22], (DEPTH, D_MODEL, 2 * W_MEM), D_MODEL ** -0.5),
        "w_out": nrm(ks[23], (DEPTH, D_MIX, D_MODEL), D_MIX ** -0.5),
        "final_g": 1.0 + nrm(ks[24], (D_MODEL,), 0.05),
    }


def reference(x_prompt, x_sample, mem_prompt, cache_k, cache_v, cache_mem_k, cache_mem_v,
              state_conv, state_pool, page_table, pre_g, post_g, w_in, conv_w_dw, conv_b_dw,
              conv_ln_g, conv_ln_b, conv_w_pw, conv_b_pw, pool_w, pool_scale, sb_bias, mem_g,
              w_mem_kv, w_out, final_g):
    bp, lp, _ = x_prompt.shape
    bs, ls, _ = x_sample.shape
    past = page_table.shape[1] * PAGE_SIZE
    pos_p = jnp.arange(lp)
    pos_s = past + jnp.arange(ls)
    k_pos_s = jnp.arange(past + ls)

    xp, xs = x_prompt, x_sample
    kp_l, vp_l, ks_l, vs_l = [], [], [], []
    cp_l, cs_l, pp_l, ps_l, mk_l, mv_l = [], [], [], [], [], []
    for l in range(DEPTH):
        lw = (pre_g[l], post_g[l], w_in[l], conv_w_dw[l], conv_b_dw[l], conv_ln_g[l], conv_ln_b[l],
              conv_w_pw[l], conv_b_pw[l], pool_w[l], pool_scale[l], sb_bias[l], w_out[l])
        mk_p, mv_p = mem_kv(mem_prompt, mem_g[l], w_mem_kv[l])
        xp, c_new, p_new, k_new, v_new = mixer_layer(
            xp, pos_p, jnp.zeros((bp, CONV_K - 1, W_CONV), xp.dtype),
            jnp.zeros((bp, POOL_HIST, W_POOL), xp.dtype), sb_prompt, mk_p, mv_p, *lw)
        kp_l.append(k_new); vp_l.append(v_new); cp_l.append(c_new); pp_l.append(p_new)
        mk_l.append(mk_p); mv_l.append(mv_p)
        k_past = cache_k[l][page_table].reshape(bs, past, SB_HEADS, SB_HEAD_DIM)
        v_past = cache_v[l][page_table].reshape(bs, past, SB_HEADS, SB_HEAD_DIM)

        def sb_sample(q, k, v, bias, k_past=k_past, v_past=v_past):
            return sb_block(q, jnp.concatenate([k_past, k], axis=1),
                            jnp.concatenate([v_past, v], axis=1), bias, pos_s, k_pos_s)

        xs, c_new, p_new, k_new, v_new = mixer_layer(
            xs, pos_s, state_conv[l], state_pool[l], sb_sample, cache_mem_k[l], cache_mem_v[l], *lw)
        ks_l.append(k_new); vs_l.append(v_new); cs_l.append(c_new); ps_l.append(p_new)

    y_prompt = rms_norm(xp, final_g)
    y_sample = rms_norm(xs, final_g)
    return (y_prompt, y_sample,
            jnp.stack(kp_l), jnp.stack(vp_l), jnp.stack(ks_l), jnp.stack(vs_l),
            jnp.stack(cp_l), jnp.stack(cs_l), jnp.stack(pp_l), jnp.stack(ps_l),
            jnp.stack(mk_l), jnp.stack(mv_l))
```

```python
import contextlib
import numpy as np
import concourse.bass as bass
import concourse.mybir as mybir
from concourse.bass_utils import run_bass_kernel_spmd

F32 = mybir.dt.float32
BF16 = mybir.dt.bfloat16
I32 = mybir.dt.int32
AF = mybir.ActivationFunctionType
ALU = mybir.AluOpType
AX = mybir.AxisListType
PE, ACT, DVE, POOL, SP = "pe", "act", "dve", "pool", "sp"

D = 1024
SEQ = 2048
DEPTH = 2
DIN = 5632
NCORE = 8
NSEQ = 4
NTOK = 16
NMEM = 256
EPS = 1e-6
POOLW = (2, 4, 8, 16)
DEBUG = False
DEBUG_Q = 0
NL = 2
BRANCHES = 'cpsm'


class Op:
    __slots__ = ("id", "eng", "fn", "deps", "dma", "chan", "sig", "count")

    def __init__(self, id, eng, fn, dma, chan):
        self.id = id
        self.eng = eng
        self.fn = fn
        self.deps = set()
        self.dma = dma
        self.chan = chan
        self.sig = False
        self.count = 0


class Sched:
    def __init__(self, nc):
        self.nc = nc
        self.ops = []
        self.last_write = {}
        self.readers = {}
        self.barrier_id = None

    def add(self, eng, fn, reads=(), writes=(), dma=False, chan=None):
        op = Op(len(self.ops), eng, fn, dma, chan)
        for r in reads:
            lw = self.last_write.get(r)
            if lw is not None:
                op.deps.add(lw)
        for w in writes:
            lw = self.last_write.get(w)
            if lw is not None:
                op.deps.add(lw)
            for rd in self.readers.get(w, ()):
                op.deps.add(rd)
        if self.barrier_id is not None:
            op.deps.add(self.barrier_id)
        op.deps.discard(op.id)
        for r in reads:
            self.readers.setdefault(r, []).append(op.id)
        for w in writes:
            self.last_write[w] = op.id
            self.readers[w] = []
        self.ops.append(op)
        return op

    def barrier(self):
        last = {}
        dmas = set()
        for op in self.ops:
            if op.dma:
                dmas.add(op.id)
            else:
                last[op.eng] = op.id
        bop = Op(len(self.ops), POOL, lambda e: e.memset(self.bar_tile[0:1, 0:1], 0.0), False, None)
        bop.deps = set(last.values()) | dmas
        self.ops.append(bop)
        self.barrier_id = bop.id

    @staticmethod
    def _needs_wait(op, d):
        if d.dma or op.dma:
            return True
        if d.eng != op.eng:
            return True
        return d.eng != PE

    NSLOT = {"w": 6, "ld": 8, "st": 8}

    def emit(self):
        nc = self.nc
        ops = self.ops
        for op in ops:
            for d in op.deps:
                if self._needs_wait(op, ops[d]):
                    ops[d].sig = True
            if op.dma:
                op.sig = True
        cnt = {}
        nuse = {}
        for op in ops:
            if not op.sig:
                continue
            if op.dma:
                k = self.NSLOT.get(op.chan, 16)
                u = nuse.get(op.chan, 0)
                nuse[op.chan] = u + 1
                key = ("c", op.chan, u % k)
            else:
                key = ("e", op.eng, 0)
            cnt[key] = cnt.get(key, 0) + (16 if op.dma else 1)
            op.count = cnt[key]
            op.chan = key if op.dma else op.chan
        with contextlib.ExitStack() as st:
            sems = {}
            for key in cnt:
                sems[key] = st.enter_context(nc.semaphore("s_%s_%s_%d" % key))
            block = st.enter_context(nc.Block())
            engs = {PE: block.tensor, ACT: block.scalar, DVE: block.vector,
                    POOL: block.gpsimd, SP: block.sync}
            per_eng = {e: [] for e in engs}
            for op in ops:
                per_eng[op.eng].append(op)

            def keyof(o):
                return o.chan if o.dma else ("e", o.eng, 0)

            def make(e):
                def body(eng):
                    waited = {}
                    for op in per_eng[e]:
                        need = {}
                        for d in op.deps:
                            dop = ops[d]
                            if not self._needs_wait(op, dop):
                                continue
                            key = keyof(dop)
                            if dop.count > need.get(key, 0):
                                need[key] = dop.count
                        if op.dma and op.count > 16:
                            key = keyof(op)
                            need[key] = max(need.get(key, 0), op.count - 16)
                        for key, v in need.items():
                            if waited.get(key, 0) >= v:
                                continue
                            eng.wait_ge(sems[key], v)
                            waited[key] = v
                        ins = op.fn(eng)
                        if op.sig:
                            ins.then_inc(sems[keyof(op)], 16 if op.dma else 1)
                    if e == SP:
                        for key, v in cnt.items():
                            if key[0] == "c" and waited.get(key, 0) < v:
                                eng.wait_ge(sems[key], v)
                return body

            for e, dec in engs.items():
                dec(make(e))


def build(NP=64, NPHYS=2560, do_prompt=True, do_sample=True):
    nc = bass.Bass("TRN2", target_bir_lowering=False)
    S = Sched(nc)

    def din(name, shape, dt=F32):
        return nc.dram_tensor(name, list(shape), dt, kind="ExternalInput").ap()

    def dout(name, shape, dt=F32):
        return nc.dram_tensor(name, list(shape), dt, kind="ExternalOutput").ap()

    xp = din("xp", [SEQ, D])
    xs = din("xs", [NTOK, D])
    memp = din("memp", [NMEM, D])
    ckv = din("ckv", [DEPTH * NPHYS * 128, 1024])
    cmk = din("cmk", [DEPTH, NSEQ, NMEM, 512])
    cmv = din("cmv", [DEPTH, NSEQ, NMEM, 512])
    stc = din("stc", [DEPTH, NSEQ, 30, 512])
    stp = din("stp", [DEPTH, NSEQ, 15, 512])
    ptab = din("ptab", [1, NSEQ * NP], I32)
    pre_g = din("pre_g", [DEPTH, D])
    post_g = din("post_g", [DEPTH, D])
    mem_g = din("mem_g", [DEPTH, D])
    final_g = din("final_g", [1, D])
    w_in = din("w_in", [DEPTH, D, DIN])
    w_mem = din("w_mem", [DEPTH, D, 1024])
    w_out = din("w_out", [DEPTH, 2048, D])
    w_pw = din("w_pw", [DEPTH, 512, 512])
    pool_w = din("pool_w", [DEPTH, 4, 128, 128])
    cvec = din("cvec", [DEPTH, 36, 512])
    sbb = din("sbb", [DEPTH, 8])

    y_p = dout("y_p", [SEQ, D])
    y_s = dout("y_s", [NTOK, D])
    k_p = dout("k_p", [DEPTH, SEQ, 512])
    v_p = dout("v_p", [DEPTH, SEQ, 512])
    k_s = dout("k_s", [DEPTH, NTOK, 512])
    v_s = dout("v_s", [DEPTH, NTOK, 512])
    conv_p = dout("conv_p", [DEPTH, 30, 512])
    conv_s = dout("conv_s", [DEPTH, NSEQ, 30, 512])
    pool_p = dout("pool_p", [DEPTH, 15, 512])
    pool_s = dout("pool_s", [DEPTH, NSEQ, 15, 512])
    mk_p = dout("mk_p", [DEPTH, NMEM, 512])
    mv_p = dout("mv_p", [DEPTH, NMEM, 512])

    dbg = dout("dbg", [128, 16, 512]) if DEBUG else None
    st = contextlib.ExitStack()

    def sb(name, shape, dt=F32):
        return st.enter_context(nc.sbuf_tensor(name, list(shape), dt))

    def ps(name, shape, dt=F32):
        return st.enter_context(nc.psum_tensor(name, list(shape), dt))

    uid = [0]

    def U(prefix):
        uid[0] += 1
        return "%s_%d" % (prefix, uid[0])

    with st:
        bank = [ps("bank%d" % i, [128, 512], F32) for i in range(7)]
        bankT = ps("bankT", [128, 1024], BF16)
        BN = ["b%d" % i for i in range(7)]
        BT = "bT"

        identf = sb("identf", [128, 128])
        identb = sb("identb", [128, 128], BF16)
        negU = sb("negU", [128, 128], BF16)
        negOnes = sb("negOnes", [128, 128], BF16)
        onesb = sb("onesb", [128, 128], BF16)
        inv512 = sb("inv512", [128, 128], BF16)
        mask01 = sb("mask01", [128, 128], BF16)
        epst = sb("epst", [128, 1])
        S.bar_tile = sb("bar_tile", [128, 1])
        tmpc = sb("tmpc", [128, 128])

        def pool_op(fn, reads=(), writes=()):
            S.add(POOL, fn, reads, writes)

        pool_op(lambda e: e.memset(identf[:], 1.0), writes=["identf"])
        pool_op(lambda e: e.affine_select(out=identf[:], in_=identf[:], pattern=[[-1, 128]],
                                          compare_op=ALU.is_equal, fill=0.0, base=0,
                                          channel_multiplier=1), ["identf"], ["identf"])
        S.add(DVE, lambda e: e.tensor_copy(out=identb[:], in_=identf[:]), ["identf"], ["identb"])
        pool_op(lambda e: e.memset(tmpc[:], -1.0), writes=["tmpc"])
        pool_op(lambda e: e.affine_select(out=tmpc[:], in_=tmpc[:], pattern=[[-1, 128]],
                                          compare_op=ALU.is_ge, fill=0.0, base=0,
                                          channel_multiplier=1), ["tmpc"], ["tmpc"])
        S.add(DVE, lambda e: e.tensor_copy(out=negU[:], in_=tmpc[:]), ["tmpc"], ["negU"])
        pool_op(lambda e: e.memset(tmpc[:], 1.0), ["tmpc"], ["tmpc"])
        pool_op(lambda e: e.affine_select(out=tmpc[:], in_=tmpc[:], pattern=[[1, 128]],
                                          compare_op=ALU.is_ge, fill=0.0, base=-1,
                                          channel_multiplier=-1), ["tmpc"], ["tmpc"])
        S.add(DVE, lambda e: e.tensor_copy(out=mask01[:], in_=tmpc[:]), ["tmpc"], ["mask01"])
        pool_op(lambda e: e.memset(negOnes[:], -1.0), writes=["negOnes"])
        pool_op(lambda e: e.memset(onesb[:], 1.0), writes=["onesb"])
        pool_op(lambda e: e.memset(inv512[:], 1.0 / 512), writes=["inv512"])
        pool_op(lambda e: e.memset(epst[:], EPS), writes=["epst"])

        gpre = sb("gpre", [128, D])
        gpost = sb("gpost", [128, D])
        gmisc = sb("gmisc", [128, D])
        ctab = sb("ctab", [128, 4, 36])
        kvst = [sb("kvst%d" % i, [128, 512]) for i in range(2)]
        cv_sb = kvst[1][0:36, :]
        sbb_t = sb("sbb_t", [128, 8])
        wpw = sb("wpw", [128, 4, 512], BF16)
        wpool = sb("wpool", [128, 4, 128], BF16)
        win = [sb("win%d" % i, [128, 8, 512], BF16) for i in range(2)]
        wout = sb("wout", [128, 16, 512], BF16)
        stat = sb("stat", [128, 16])
        hs = [sb("hs%d" % i, [128, D], BF16) for i in range(2)]
        sqj = sb("sqj", [128, D], BF16)

        def load_layer_consts(l):
            S.add(SP, lambda e: e.dma_start(out=gpre[:], in_=pre_g[l:l + 1, :].partition_broadcast(128)),
                  writes=["gpre"], dma=True, chan="ld")
            S.add(SP, lambda e: e.dma_start(out=gpost[:], in_=post_g[l:l + 1, :].partition_broadcast(128)),
                  writes=["gpost"], dma=True, chan="ld")
            S.add(SP, lambda e: e.dma_start(out=sbb_t[:], in_=sbb[l:l + 1, :].partition_broadcast(128)),
                  writes=["sbb_t"], dma=True, chan="ld")
            S.add(SP, lambda e: e.dma_start(out=cv_sb[:], in_=cvec[l]), writes=["kvst1"], dma=True, chan="ld")
            for ch in range(4):
                b = BN[ch % 2]
                S.add(PE, lambda e, ch=ch: e.transpose(out=bank[ch % 2][:, 0:36], in_=cv_sb[:, ch * 128:(ch + 1) * 128],
                                                       identity=identf[0:36, 0:36]),
                      ["kvst1", "identf"], [b])
                S.add(DVE, lambda e, ch=ch: e.tensor_copy(out=ctab[:, ch, :], in_=bank[ch % 2][:, 0:36]), [b], ["ctab"])
            S.add(POOL, lambda e: e.dma_start(out=wpw[:], in_=w_pw[l].rearrange("(q p) f -> p q f", p=128)),
                  writes=["wpw"], dma=True, chan="w")
            S.add(POOL, lambda e: e.dma_start(out=wpool[:], in_=pool_w[l].rearrange("g c d -> c g d")),
                  writes=["wpool"], dma=True, chan="w")

        wctr = [0]

        def load_win(l, col0, ncols=512):
            i = wctr[0] % 2
            wctr[0] += 1
            S.add(POOL, lambda e: e.dma_start(out=win[i][:, :, 0:ncols],
                                              in_=w_in[l][:, col0:col0 + ncols].rearrange("(q p) f -> p q f", p=128)),
                  writes=["win%d" % i], dma=True, chan="w")
            return i

        def rstd_of(x_ap, npart, scale_n, col):
            S.add(ACT, lambda e: e.activation(out=sqj[0:npart, 0:x_ap.shape[1]], in_=x_ap, func=AF.Square,
                                              accum_out=stat[0:npart, col:col + 1]),
                  ["xres"], ["sqj", "stat"])
            S.add(ACT, lambda e: e.activation(out=stat[0:npart, col:col + 1], in_=stat[0:npart, col:col + 1],
                                              func=AF.Ln, scale=1.0 / scale_n, bias=epst[0:npart, :]),
                  ["stat", "epst"], ["stat"])
            S.add(ACT, lambda e: e.activation(out=stat[0:npart, col:col + 1], in_=stat[0:npart, col:col + 1],
                                              func=AF.Exp, scale=-0.5), ["stat"], ["stat"])

        if do_sample:
          with contextlib.ExitStack() as st2:
            def sb2(name, shape, dt=F32):
                return st2.enter_context(nc.sbuf_tensor(name, list(shape), dt))

            GP = min(8, NP)
            NG = NP // GP
            W = GP * 32
            xs_t = sb2("xs_t", [16, D])
            os_s = sb2("os_s", [16, D])
            hsT = sb2("hsT_s", [128, 8, 16], BF16)
            u_ext_s = sb2("u_ext_s", [128, 4, NSEQ, 34])
            pu_ext_s = sb2("pu_ext_s", [128, 4, NSEQ, 19])
            hist_c = sb2("hist_c", [30, NSEQ, 512])
            hist_p = sb2("hist_p", [15, NSEQ, 512])
            gate_s = sb2("gate_s", [128, 4, 16], BF16)
            ycs = sb2("ycs", [128, 4, 16])
            lnb_s = sb2("lnb_s", [128, 2, 16], BF16)
            stats_s = sb2("stats_s", [128, 2, 16])
            sconv_s = sb2("sconv_s", [128, 4, 16], BF16)
            ptmp_s = [sb2("ptmp_s%d" % i, [128, NSEQ, 19]) for i in range(2)]
            dT_s = sb2("dT_s", [128, 4, 16], BF16)
            mixTs = sb2("mixTs", [128, 16, 16], BF16)
            qTs = sb2("qTs", [128, 4, 16], BF16)
            kTs = sb2("kTs", [128, 4, 16], BF16)
            mqTs = sb2("mqTs", [128, 4, 16], BF16)
            v_tok = sb2("v_tok", [16, 512], BF16)
            Qblk = sb2("Qblk", [128, 4, NSEQ, 8], BF16)
            bias32 = sb2("bias32", [128, 32])
            bias_rep = sb2("bias_rep", [128, W])
            maskf = sb2("maskf", [16, NSEQ * 32])
            masknew = sb2("masknew", [16, NSEQ, 32], BF16)
            ptab_i = sb2("ptab_i", [128, NSEQ * NP], I32)
            ptab_f = sb2("ptab_f", [128, NSEQ * NP])
            rows_i = sb2("rows_i", [128, NSEQ * NP], I32)
            iot_i = sb2("iot_i", [128, 1], I32)
            iot_f = sb2("iot_f", [128, 2])
            kvpg = [sb2("kvpg%d" % i, [128, 1024], BF16) for i in range(2 * GP)]
            kTpg = [sb2("kTpg%d" % i, [128, 8, 128], BF16) for i in range(2)]
            zsb = sb2("zsb", [128, W])
            es = sb2("es", [128, W])
            sps = sb2("sps", [128, W], BF16)
            scan = [sb2("scan%d" % i, [128, 2 * W]) for i in range(2)]
            args_t = sb2("args_t", [128, W])
            aTs = sb2("aTs", [128, W], BF16)
            C_in = sb2("C_in", [128, 32])
            ysT = sb2("ysT", [128, 4, 16])
            mkb = sb2("mkb", [128, 2, 512], BF16)
            mvbs = sb2("mvbs", [128, 2, 512], BF16)
            mkTs = sb2("mkTs", [128, 4, NMEM], BF16)
            pTs = sb2("pTs", [128, 32], BF16)
            rden_s = sb2("rden_s", [128, 16])
            g2_s = sb2("g2_s", [128, 16])

            S.add(SP, lambda e: e.dma_start(out=xs_t[:], in_=xs), writes=["xs_t"], dma=True, chan="ld")
            S.add(SP, lambda e: e.dma_start(out=ptab_i[:], in_=ptab.partition_broadcast(128)), writes=["ptab_i"], dma=True, chan="ld")
            S.add(DVE, lambda e: e.tensor_copy(out=ptab_f[:], in_=ptab_i[:]), ["ptab_i"], ["ptab_f"])
            pool_op(lambda e: e.iota(iot_i[:], pattern=[[0, 1]], base=0, channel_multiplier=1), writes=["iot_i"])
            S.add(DVE, lambda e: e.tensor_copy(out=iot_f[:, 0:1], in_=iot_i[:]), ["iot_i"], ["iot_f"])
            S.add(DVE, lambda e: e.tensor_scalar(out=iot_f[:, 1:2], in0=iot_f[:, 0:1], scalar1=float(NPHYS * 128), scalar2=None,
                                                 op0=ALU.add), ["iot_f"], ["iot_f"])
            pool_op(lambda e: e.memset(maskf[:], 1.0), writes=["maskf"])
            pool_op(lambda e: e.affine_select(out=maskf[:].rearrange("p (b h t) -> p b h t", b=NSEQ, h=8), in_=maskf[:].rearrange("p (b h t) -> p b h t", b=NSEQ, h=8),
                                              pattern=[[4, NSEQ], [0, 8], [1, 4]], compare_op=ALU.is_ge, fill=0.0, base=-1,
                                              channel_multiplier=-1), ["maskf"], ["maskf"])
            pool_op(lambda e: e.affine_select(out=maskf[:].rearrange("p (b h t) -> p b h t", b=NSEQ, h=8), in_=maskf[:].rearrange("p (b h t) -> p b h t", b=NSEQ, h=8),
                                              pattern=[[-4, NSEQ], [0, 8], [0, 4]], compare_op=ALU.is_ge, fill=0.0, base=0,
                                              channel_multiplier=1), ["maskf"], ["maskf"])
            S.add(DVE, lambda e: e.tensor_copy(out=masknew[:].rearrange("p b c -> p (b c)"), in_=maskf[:]), ["maskf"], ["masknew"])
            for i in range(2):
                pool_op(lambda e, i=i: e.memset(scan[i][:], 0.0), writes=["scan%d" % i])

            def v4(ap16):
                return ap16.rearrange("p (b t) -> p b t", b=NSEQ)

            def sproj(l, col0, evac):
                i = load_win(l, col0)
                wn = "win%d" % i
                for ch in range(4):
                    b = ch % 2
                    for kc in range(8):
                        S.add(PE, lambda e, i=i, ch=ch, kc=kc, b=b: e.matmul(
                            bank[b][:, 0:16], lhsT=win[i][:, kc, ch * 128:(ch + 1) * 128], rhs=hsT[:, kc, :],
                            start=(kc == 0), stop=(kc == 7), skip_group_check=True), ["hsT", wn], [BN[b]])
                    evac(ch, b)
                return i

            def sproj_tok(l, i, dst, keep=None):
                wn = "win%d" % i
                for kc in range(8):
                    S.add(PE, lambda e, kc=kc: e.matmul(bank[0][0:16, :], lhsT=hsT[:, kc, :], rhs=win[i][:, kc, :],
                                                        start=(kc == 0), stop=(kc == 7), skip_group_check=True), ["hsT", wn], [BN[0]])
                S.add(ACT, lambda e: e.activation(out=kvst[0][0:16, :], in_=bank[0][0:16, :], func=AF.Copy), [BN[0]], ["kvst0"])
                S.add(SP, lambda e: e.dma_start(out=dst[l], in_=kvst[0][0:16, :]), ["kvst0"], dma=True, chan="st")
                if keep is not None:
                    S.add(DVE, lambda e: e.tensor_copy(out=keep[:], in_=kvst[0][0:16, :]), ["kvst0"], ["v_tok"])

            def sample_layer(l):
                load_layer_consts(l)
                S.add(ACT, lambda e: e.activation(out=sqj[0:16, :], in_=xs_t[:], func=AF.Square, accum_out=stat[0:16, 0:1]),
                      ["xs_t"], ["sqj", "stat"])
                S.add(ACT, lambda e: e.activation(out=stat[0:16, 0:1], in_=stat[0:16, 0:1], func=AF.Ln, scale=1.0 / D,
                                                  bias=epst[0:16, :]), ["stat", "epst"], ["stat"])
                S.add(ACT, lambda e: e.activation(out=stat[0:16, 0:1], in_=stat[0:16, 0:1], func=AF.Exp, scale=-0.5), ["stat"], ["stat"])
                S.add(DVE, lambda e: e.scalar_tensor_tensor(out=hs[0][0:16, :], in0=xs_t[:], scalar=stat[0:16, 0:1], in1=gpre[0:16, :],
                                                            op0=ALU.mult, op1=ALU.mult), ["xs_t", "stat", "gpre"], ["hs0"])
                for kc in range(8):
                    S.add(PE, lambda e, kc=kc: e.transpose(out=bankT[:, kc * 16:(kc + 1) * 16], in_=hs[0][0:16, kc * 128:(kc + 1) * 128],
                                                           identity=identb[0:16, 0:16]), ["hs0", "identb"], [BT])
                S.add(ACT, lambda e: e.activation(out=hsT[:], in_=bankT[:, 0:128].rearrange("p (k t) -> p k t", k=8), func=AF.Copy),
                      [BT], ["hsT"])

                S.add(SP, lambda e: e.dma_start(out=hist_c[:], in_=stc[l].rearrange("b k c -> k b c")), writes=["hist_c"], dma=True, chan="ld")
                for b in range(NSEQ):
                    for ch in range(4):
                        S.add(PE, lambda e, b=b, ch=ch: e.transpose(out=bank[0][:, (b * 4 + ch) * 32:(b * 4 + ch) * 32 + 30],
                                                                   in_=hist_c[0:30, b, ch * 128:(ch + 1) * 128], identity=identf[0:30, 0:30]),
                              ["hist_c", "identf"], [BN[0]])
                for b in range(NSEQ):
                    S.add(ACT, lambda e, b=b: e.activation(out=u_ext_s[:, :, b, 0:30],
                                                          in_=bank[0][:, b * 128:(b + 1) * 128].rearrange("p (c k) -> p c k", c=4)[:, :, 0:30],
                                                          func=AF.Copy), [BN[0]], ["u_ext_s"])
                sproj(l, 512, lambda ch, b: S.add(
                    ACT, lambda e: e.activation(out=u_ext_s[:, ch, :, 30:34], in_=v4(bank[b][:, 0:16]), func=AF.Sigmoid),
                    [BN[b]], ["u_ext_s"]))
                sproj(l, 0, lambda ch, b: S.add(
                    DVE, lambda e: e.tensor_tensor(out=u_ext_s[:, ch, :, 30:34], in0=v4(bank[b][:, 0:16]), in1=u_ext_s[:, ch, :, 30:34],
                                                   op=ALU.mult), [BN[b], "u_ext_s"], ["u_ext_s"]))
                sproj(l, 1024, lambda ch, b: S.add(
                    ACT, lambda e: e.activation(out=gate_s[:, ch, :], in_=bank[b][:, 0:16], func=AF.Silu), [BN[b]], ["gate_s"]))
                for ch in range(4):
                    S.add(DVE, lambda e, ch=ch: e.tensor_scalar(out=v4(ycs[:, ch, :]), in0=u_ext_s[:, ch, :, 0:4], scalar1=ctab[:, ch, 0:1],
                                                               scalar2=ctab[:, ch, 31:32], op0=ALU.mult, op1=ALU.add),
                          ["u_ext_s", "ctab"], ["ycs"])
                    for k in range(1, 31):
                        S.add(DVE, lambda e, ch=ch, k=k: e.scalar_tensor_tensor(
                            out=v4(ycs[:, ch, :]), in0=u_ext_s[:, ch, :, k:k + 4], scalar=ctab[:, ch, k:k + 1], in1=v4(ycs[:, ch, :]),
                            op0=ALU.mult, op1=ALU.add), ["u_ext_s", "ctab", "ycs"], ["ycs"])
                for ch in range(4):
                    S.add(ACT, lambda e, ch=ch: e.activation(out=lnb_s[:, 0, :], in_=ycs[:, ch, :], func=AF.Copy), ["ycs"], ["lnb_s0"])
                    S.add(ACT, lambda e, ch=ch: e.activation(out=lnb_s[:, 1, :], in_=ycs[:, ch, :], func=AF.Square), ["ycs"], ["lnb_s1"])
                    S.add(PE, lambda e, ch=ch: e.matmul(bank[0][:, 0:16], lhsT=inv512[:], rhs=lnb_s[:, 0, :], start=(ch == 0), stop=(ch == 3),
                                                        skip_group_check=True), ["lnb_s0", "inv512"], [BN[0]])
                    S.add(PE, lambda e, ch=ch: e.matmul(bank[1][:, 0:16], lhsT=inv512[:], rhs=lnb_s[:, 1, :], start=(ch == 0), stop=(ch == 3),
                                                        skip_group_check=True), ["lnb_s1", "inv512"], [BN[1]])
                S.add(ACT, lambda e: e.activation(out=stats_s[:, 0, :], in_=bank[0][:, 0:16], func=AF.Copy), [BN[0]], ["stats_s"])
                S.add(ACT, lambda e: e.activation(out=stats_s[:, 1, :], in_=bank[0][:, 0:16], func=AF.Square), [BN[0]], ["stats_s"])
                S.add(DVE, lambda e: e.tensor_tensor(out=stats_s[:, 1, :], in0=bank[1][:, 0:16], in1=stats_s[:, 1, :], op=ALU.subtract),
                      [BN[1], "stats_s"], ["stats_s"])
                S.add(ACT, lambda e: e.activation(out=stats_s[:, 1, :], in_=stats_s[:, 1, :], func=AF.Ln, bias=epst[:]), ["stats_s", "epst"], ["stats_s"])
                S.add(ACT, lambda e: e.activation(out=stats_s[:, 1, :], in_=stats_s[:, 1, :], func=AF.Exp, scale=-0.5), ["stats_s"], ["stats_s"])
                for ch in range(4):
                    S.add(DVE, lambda e, ch=ch: e.tensor_tensor(out=ycs[:, ch, :], in0=ycs[:, ch, :], in1=stats_s[:, 0, :], op=ALU.subtract),
                          ["ycs", "stats_s"], ["ycs"])
                    S.add(DVE, lambda e, ch=ch: e.tensor_tensor(out=ycs[:, ch, :], in0=ycs[:, ch, :], in1=stats_s[:, 1, :], op=ALU.mult),
                          ["ycs", "stats_s"], ["ycs"])
                    S.add(ACT, lambda e, ch=ch: e.activation(out=sconv_s[:, ch, :], in_=ycs[:, ch, :], func=AF.Silu,
                                                            scale=ctab[:, ch, 32:33], bias=ctab[:, ch, 33:34]), ["ycs", "ctab"], ["sconv_s"])
                for co in range(4):
                    b = co % 2
                    for ch in range(4):
                        S.add(PE, lambda e, co=co, ch=ch, b=b: e.matmul(bank[b][:, 0:16], lhsT=wpw[:, ch, co * 128:(co + 1) * 128],
                                                                        rhs=sconv_s[:, ch, :], start=(ch == 0), stop=(ch == 3),
                                                                        skip_group_check=True), ["sconv_s", "wpw"], [BN[b]])
                    S.add(DVE, lambda e, co=co, b=b: e.scalar_tensor_tensor(out=mixTs[:, co, :], in0=bank[b][:, 0:16], scalar=ctab[:, co, 34:35],
                                                                            in1=gate_s[:, co, :], op0=ALU.add, op1=ALU.mult),
                          [BN[b], "ctab", "gate_s"], ["mixTs"])
                for b in range(NSEQ):
                    for ch in range(4):
                        S.add(PE, lambda e, b=b, ch=ch: e.transpose(out=bank[1][0:30, ch * 128:(ch + 1) * 128], in_=u_ext_s[:, ch, b, 4:34],
                                                                   identity=identf[:]), ["u_ext_s", "identf"], [BN[1]])
                    S.add(ACT, lambda e: e.activation(out=kvst[1][0:30, :], in_=bank[1][0:30, :], func=AF.Copy), [BN[1]], ["kvst1"])
                    S.add(SP, lambda e, b=b: e.dma_start(out=conv_s[l, b], in_=kvst[1][0:30, :]), ["kvst1"], dma=True, chan="st")

                S.add(SP, lambda e: e.dma_start(out=hist_p[:], in_=stp[l].rearrange("b k c -> k b c")), writes=["hist_p"], dma=True, chan="ld")
                for b in range(NSEQ):
                    for ch in range(4):
                        S.add(PE, lambda e, b=b, ch=ch: e.transpose(out=bank[0][:, (b * 4 + ch) * 16:(b * 4 + ch) * 16 + 15],
                                                                   in_=hist_p[0:15, b, ch * 128:(ch + 1) * 128], identity=identf[0:15, 0:15]),
                              ["hist_p", "identf"], [BN[0]])
                for b in range(NSEQ):
                    S.add(ACT, lambda e, b=b: e.activation(out=pu_ext_s[:, :, b, 0:15],
                                                          in_=bank[0][:, b * 64:(b + 1) * 64].rearrange("p (c k) -> p c k", c=4)[:, :, 0:15],
                                                          func=AF.Copy), [BN[0]], ["pu_ext_s"])
                sproj(l, 1536, lambda ch, b: S.add(
                    ACT, lambda e: e.activation(out=pu_ext_s[:, ch, :, 15:19], in_=v4(bank[b][:, 0:16]), func=AF.Copy), [BN[b]], ["pu_ext_s"]))
                sproj(l, 2048, lambda ch, b: S.add(
                    ACT, lambda e: e.activation(out=gate_s[:, ch, :], in_=bank[b][:, 0:16], func=AF.Silu), [BN[b]], ["gate_s"]))
                for g, w in enumerate(POOLW):
                    cur = pu_ext_s[:, g, :, :]
                    curn = "pu_ext_s"
                    lo, d, k = 0, 1, 0
                    while d < w:
                        nxt = ptmp_s[k % 2][:]
                        nn = "ptmp_s%d" % (k % 2)
                        pool_op(lambda e, cur=cur, nxt=nxt, lo=lo, d=d: e.tensor_tensor(
                            out=nxt[:, :, lo + d:19], in0=cur[:, :, lo + d:19], in1=cur[:, :, lo:19 - d], op=ALU.add), [curn], [nn])
                        cur, curn = nxt, nn
                        lo += d
                        d *= 2
                        k += 1
                    S.add(DVE, lambda e, cur=cur, g=g, w=w: e.scalar_tensor_tensor(
                        out=v4(dT_s[:, g, :]), in0=cur[:, :, 15:19], scalar=1.0 / w, in1=pu_ext_s[:, g, :, 15:19],
                        op0=ALU.mult, op1=ALU.subtract), [curn, "pu_ext_s"], ["dT_s"])
                    b = g % 2
                    S.add(PE, lambda e, g=g, b=b: e.matmul(bank[b][:, 0:16], lhsT=wpool[:, g, :], rhs=dT_s[:, g, :], start=True, stop=True,
                                                           skip_group_check=True), ["dT_s", "wpool"], [BN[b]])
                    S.add(DVE, lambda e, g=g, b=b: e.scalar_tensor_tensor(out=mixTs[:, 4 + g, :], in0=bank[b][:, 0:16], scalar=ctab[:, g, 35:36],
                                                                          in1=gate_s[:, g, :], op0=ALU.mult, op1=ALU.mult),
                          [BN[b], "ctab", "gate_s"], ["mixTs"])
                for b in range(NSEQ):
                    for ch in range(4):
                        S.add(PE, lambda e, b=b, ch=ch: e.transpose(out=bank[1][0:15, ch * 128:(ch + 1) * 128], in_=pu_ext_s[:, ch, b, 4:19],
                                                                   identity=identf[:]), ["pu_ext_s", "identf"], [BN[1]])
                    S.add(ACT, lambda e: e.activation(out=kvst[1][0:15, :], in_=bank[1][0:15, :], func=AF.Copy), [BN[1]], ["kvst1"])
                    S.add(SP, lambda e, b=b: e.dma_start(out=pool_s[l, b], in_=kvst[1][0:15, :]), ["kvst1"], dma=True, chan="st")

                sproj(l, 4608, lambda ch, b: S.add(
                    ACT, lambda e: e.activation(out=mqTs[:, ch, :], in_=bank[b][:, 0:16], func=AF.Copy, scale=128 ** -0.5), [BN[b]], ["mqTs"]))
                sproj(l, 5120, lambda ch, b: S.add(
                    ACT, lambda e: e.activation(out=gate_s[:, ch, :], in_=bank[b][:, 0:16], func=AF.Silu), [BN[b]], ["gate_s"]))
                for b in range(NSEQ):
                    S.add(POOL, lambda e, b=b: e.dma_start(out=mkb[:], in_=cmk[l, b].rearrange("(c m) f -> m c f", c=2)),
                          writes=["mkb"], dma=True, chan="w")
                    S.add(POOL, lambda e, b=b: e.dma_start(out=mvbs[:], in_=cmv[l, b].rearrange("(c m) f -> m c f", c=2)),
                          writes=["mvbs"], dma=True, chan="w")
                    for hh in range(4):
                        for mc in range(2):
                            S.add(PE, lambda e, hh=hh, mc=mc: e.transpose(out=bankT[:, (hh * 2 + mc) * 128:(hh * 2 + mc + 1) * 128],
                                                                         in_=mkb[:, mc, hh * 128:(hh + 1) * 128], identity=identb[:]),
                                  ["mkb", "identb"], [BT])
                    S.add(ACT, lambda e: e.activation(out=mkTs[:].rearrange("p h m -> p (h m)"), in_=bankT[:], func=AF.Copy), [BT], ["mkTs"])
                    for hh in range(4):
                        for mc in range(2):
                            c = (hh * 2 + mc) * 4
                            S.add(PE, lambda e, hh=hh, mc=mc, c=c, b=b: e.matmul(
                                bank[2][:, c:c + 4], lhsT=mkTs[:, hh, mc * 128:(mc + 1) * 128], rhs=mqTs[:, hh, b * 4:(b + 1) * 4],
                                start=True, stop=True, skip_group_check=True), ["mkTs", "mqTs"], [BN[2]])
                    S.add(ACT, lambda e: e.activation(out=pTs[:], in_=bank[2][:, 0:32], func=AF.Exp), [BN[2]], ["pTs"])
                    for hh in range(4):
                        for mc in range(2):
                            c = (hh * 2 + mc) * 4
                            S.add(PE, lambda e, hh=hh, mc=mc, c=c: e.matmul(
                                bank[4][:, hh * 4:(hh + 1) * 4], lhsT=mvbs[:, mc, hh * 128:(hh + 1) * 128], rhs=pTs[:, c:c + 4],
                                start=(mc == 0), stop=(mc == 1), skip_group_check=True), ["mvbs", "pTs"], [BN[4]])
                            S.add(PE, lambda e, hh=hh, mc=mc, c=c: e.matmul(
                                bank[5][:, hh * 4:(hh + 1) * 4], lhsT=onesb[:], rhs=pTs[:, c:c + 4],
                                start=(mc == 0), stop=(mc == 1), skip_group_check=True), ["onesb", "pTs"], [BN[5]])
                    S.add(DVE, lambda e: e.reciprocal(out=rden_s[:], in_=bank[5][:, 0:16]), [BN[5]], ["rden_s"])
                    pool_op(lambda e, b=b: e.tensor_tensor(out=g2_s[:].rearrange("p (h t) -> p h t", h=4),
                                                           in0=rden_s[:].rearrange("p (h t) -> p h t", h=4),
                                                           in1=gate_s[:, :, b * 4:(b + 1) * 4], op=ALU.mult), ["rden_s", "gate_s"], ["g2_s"])
                    S.add(DVE, lambda e, b=b: e.tensor_tensor(out=mixTs[:, 12:16, b * 4:(b + 1) * 4],
                                                             in0=bank[4][:, 0:16].rearrange("p (h t) -> p h t", h=4),
                                                             in1=g2_s[:].rearrange("p (h t) -> p h t", h=4), op=ALU.mult),
                          [BN[4], "g2_s"], ["mixTs"])

                sproj(l, 2560, lambda ch, b: S.add(
                    ACT, lambda e: e.activation(out=qTs[:, ch, :], in_=bank[b][:, 0:16], func=AF.Copy, scale=0.125), [BN[b]], ["qTs"]))
                iw = sproj(l, 3072, lambda ch, b: S.add(
                    ACT, lambda e: e.activation(out=kTs[:, ch, :], in_=bank[b][:, 0:16], func=AF.Copy), [BN[b]], ["kTs"]))
                sproj_tok(l, iw, k_s)
                iw = load_win(l, 3584)
                sproj_tok(l, iw, v_s, keep=v_tok)
                sproj(l, 4096, lambda ch, b: S.add(
                    ACT, lambda e: e.activation(out=gate_s[:, ch, :], in_=bank[b][:, 0:16], func=AF.Silu), [BN[b]], ["gate_s"]))
                pool_op(lambda e: e.memset(Qblk[:], 0.0), ["Qblk"], ["Qblk"])
                S.add(DVE, lambda e: e.tensor_copy(out=Qblk[0:64, :, :, 0:4], in_=qTs[0:64, :, :].rearrange("p c (b t) -> p c b t", b=NSEQ)),
                      ["qTs", "Qblk"], ["Qblk"])
                S.add(DVE, lambda e: e.tensor_copy(out=Qblk[64:128, :, :, 4:8], in_=qTs[64:128, :, :].rearrange("p c (b t) -> p c b t", b=NSEQ)),
                      ["qTs", "Qblk"], ["Qblk"])
                for t in range(4):
                    S.add(DVE, lambda e, t=t: e.tensor_copy(out=bias32[:].rearrange("p (h t) -> p h t", t=4)[:, :, t], in_=sbb_t[:, :]),
                          ["sbb_t", "bias32"], ["bias32"])
                for jj in range(GP):
                    S.add(DVE, lambda e, jj=jj: e.tensor_copy(out=bias_rep[:, jj * 32:(jj + 1) * 32], in_=bias32[:]), ["bias32", "bias_rep"], ["bias_rep"])
                S.add(DVE, lambda e: e.tensor_scalar(out=rows_f[:], in0=ptab_f[:], scalar1=128.0, scalar2=iot_f[:, l:l + 1],
                                                     op0=ALU.mult, op1=ALU.add), ["ptab_f", "iot_f"], ["rows_f"])
                S.add(DVE, lambda e: e.tensor_copy(out=rows_i[:], in_=rows_f[:]), ["rows_f"], ["rows_i"])

                def new_block(b):
                    for hp in range(4):
                        S.add(PE, lambda e, hp=hp: e.matmul(bank[4][0:16, hp * 8:(hp + 1) * 8], lhsT=kTs[:, hp, :], rhs=Qblk[:, hp, b, :],
                                                            start=True, stop=True, skip_group_check=True), ["kTs", "Qblk"], [BN[4]])
                    S.add(DVE, lambda e: e.tensor_tensor(out=zsb[0:16, 0:32], in0=bank[4][0:16, 0:32], in1=bias32[0:16, :], op=ALU.add),
                          [BN[4], "bias32"], ["zsb"])
                    S.add(ACT, lambda e: e.activation(out=es[0:16, 0:32], in_=zsb[0:16, 0:32], func=AF.Exp), ["zsb"], ["es"])
                    S.add(ACT, lambda e: e.activation(out=sps[0:16, 0:32], in_=es[0:16, 0:32], func=AF.Ln, bias=1.0), ["es"], ["sps"])
                    pool_op(lambda e: e.tensor_tensor(out=sps[0:16, 0:32], in0=sps[0:16, 0:32], in1=masknew[:, b, :], op=ALU.mult),
                            ["sps", "masknew"], ["sps"])
                    S.add(PE, lambda e: e.matmul(bank[4][0:16, 0:32], lhsT=negU[0:16, 0:16], rhs=sps[0:16, 0:32], start=True, stop=True,
                                                 skip_group_check=True), ["sps", "negU", "zsb"], [BN[4]])
                    S.add(PE, lambda e: e.matmul(bank[5][:, 0:32], lhsT=onesb[0:16, :], rhs=sps[0:16, 0:32], start=True, stop=True,
                                                 skip_group_check=True), ["sps", "onesb"], [BN[5]])
                    S.add(DVE, lambda e: e.tensor_tensor(out=args_t[0:16, 0:32], in0=bank[4][0:16, 0:32], in1=zsb[0:16, 0:32], op=ALU.add),
                          [BN[4], "zsb"], ["args_t"])
                    S.add(ACT, lambda e: e.activation(out=aTs[0:16, 0:32], in_=args_t[0:16, 0:32], func=AF.Exp), ["args_t"], ["aTs"])
                    pool_op(lambda e: e.tensor_tensor(out=aTs[0:16, 0:32], in0=aTs[0:16, 0:32], in1=masknew[:, b, :], op=ALU.mult),
                            ["aTs", "masknew"], ["aTs"])
                    for hp in range(4):
                        S.add(PE, lambda e, hp=hp: e.matmul(bank[6][:, hp * 8:(hp + 1) * 8], lhsT=v_tok[0:16, hp * 128:(hp + 1) * 128],
                                                            rhs=aTs[0:16, hp * 8:(hp + 1) * 8], start=(hp == 0), stop=False,
                                                            skip_group_check=True), ["v_tok", "aTs"], [BN[6]])
                    S.add(DVE, lambda e: e.tensor_copy(out=C_in[:], in_=bank[5][:, 0:32]), [BN[5]], ["C_in"])

                def scores(u):
                    b, G, gi = u
                    zb = 2 + gi % 2
                    for jp in range(0, GP, 2):
                        pages = [jp, jp + 1] if jp + 1 < GP else [jp]
                        kps = []
                        for jj in pages:
                            col = b * NP + G * GP + jj
                            vi = (gi % 2) * GP + jj
                            kp, kn = kvpg[vi], "kvpg%d" % vi
                            S.add(POOL, lambda e, kp=kp, col=col: e.indirect_dma_start(
                                out=kp[:], out_offset=None, in_=ckv, in_offset=bass.IndirectOffsetOnAxis(ap=rows_i[:, col:col + 1], axis=0)),
                                ["rows_i"], [kn], dma=True, chan="kv%d" % l)
                            kps.append((kp, kn))
                        for pi, (kp, kn) in enumerate(kps):
                            for hp in range(4):
                                o = (pi * 4 + hp) * 128
                                S.add(PE, lambda e, kp=kp, hp=hp, o=o: e.transpose(out=bankT[:, o:o + 128], in_=kp[:, hp * 128:(hp + 1) * 128],
                                                                                identity=identb[:]), [kn, "identb"], [BT])
                        kt, ktn = kTpg[ring[1] % 2], "kTpg%d" % (ring[1] % 2)
                        ev = ACT if ring[1] % 2 == 0 else DVE
                        ring[1] += 1
                        ncol = 512 * len(pages)
                        if ev == ACT:
                            S.add(ACT, lambda e, kt=kt, ncol=ncol: e.activation(out=kt[:].rearrange("p c s -> p (c s)")[:, 0:ncol],
                                                                            in_=bankT[:, 0:ncol], func=AF.Copy), [BT], [ktn])
                        else:
                            S.add(DVE, lambda e, kt=kt, ncol=ncol: e.tensor_copy(out=kt[:].rearrange("p c s -> p (c s)")[:, 0:ncol],
                                                                             in_=bankT[:, 0:ncol]), [BT], [ktn])
                        for pi, jj in enumerate(pages):
                            for hp in range(4):
                                c = jj * 32 + hp * 8
                                S.add(PE, lambda e, kt=kt, hp=hp, c=c, pi=pi: e.matmul(bank[zb][:, c:c + 8], lhsT=kt[:, pi * 4 + hp, :],
                                                                                    rhs=Qblk[:, hp, b, :], start=True, stop=True,
                                                                                    skip_group_check=True), [ktn, "Qblk"], [BN[zb]])

                def chain_av(u):
                    b, G, gi = u
                    zb = 2 + gi % 2
                    S.add(DVE, lambda e: e.tensor_tensor(out=zsb[:], in0=bank[zb][:, 0:W], in1=bias_rep[:], op=ALU.add),
                          [BN[zb], "bias_rep"], ["zsb"])
                    S.add(ACT, lambda e: e.activation(out=es[:], in_=zsb[:], func=AF.Exp), ["zsb"], ["es"])
                    S.add(ACT, lambda e: e.activation(out=sps[:], in_=es[:], func=AF.Ln, bias=1.0), ["es"], ["sps"])
                    S.add(PE, lambda e: e.matmul(bank[4][:, 0:W], lhsT=negU[:], rhs=sps[:], start=True, stop=True, skip_group_check=True),
                          ["sps", "negU"], [BN[4]])
                    S.add(PE, lambda e: e.matmul(bank[5][:, 0:W], lhsT=onesb[:], rhs=sps[:], start=True, stop=True, skip_group_check=True),
                          ["sps", "onesb"], [BN[5]])
                    if GP > 1:
                        S.add(DVE, lambda e: e.tensor_copy(out=scan[0][:, 0:W - 32], in_=bank[5][:, 32:W]), [BN[5], "scan0"], ["scan0"])
                    S.add(DVE, lambda e: e.tensor_copy(out=scan[0][:, W - 32:W], in_=C_in[:]), ["C_in", "scan0"], ["scan0"])
                    src, d = 0, 1
                    while d < GP:
                        S.add(DVE, lambda e, src=src, d=d: e.tensor_tensor(out=scan[1 - src][:, 0:W], in0=scan[src][:, 0:W],
                                                                          in1=scan[src][:, d * 32:W + d * 32], op=ALU.add),
                              ["scan%d" % src, "scan%d" % (1 - src)], ["scan%d" % (1 - src)])
                        src = 1 - src
                        d *= 2
                    sf, sfn = scan[src], "scan%d" % src
                    S.add(DVE, lambda e, sf=sf: e.tensor_tensor(out=C_in[:], in0=bank[5][:, 0:32], in1=sf[:, 0:32], op=ALU.add),
                          [BN[5], sfn, "C_in"], ["C_in"])
                    S.add(DVE, lambda e, sf=sf: e.tensor_tensor(out=args_t[:], in0=bank[4][:, 0:W], in1=sf[:, 0:W], op=ALU.subtract),
                          [BN[4], sfn], ["args_t"])
                    pool_op(lambda e: e.tensor_tensor(out=args_t[:], in0=args_t[:], in1=zsb[:], op=ALU.add), ["args_t", "zsb"], ["args_t"])
                    S.add(ACT, lambda e: e.activation(out=aTs[:], in_=args_t[:], func=AF.Exp), ["args_t"], ["aTs"])
                    for jj in range(GP):
                        vi = (gi % 2) * GP + jj
                        for hp in range(4):
                            c = jj * 32 + hp * 8
                            S.add(PE, lambda e, vi=vi, hp=hp, c=c: e.matmul(bank[6][:, hp * 8:(hp + 1) * 8], lhsT=kvpg[vi][:, 512 + hp * 128:512 + (hp + 1) * 128],
                                                                          rhs=aTs[:, c:c + 8], start=False, stop=False, skip_group_check=True),
                                  ["kvpg%d" % vi, "aTs"], [BN[6]])

                def extract(b):
                    S.add(ACT, lambda e: e.activation(out=ysT[0:64, :, b * 4:(b + 1) * 4],
                                                      in_=bank[6][0:64, 0:32].rearrange("p (c x) -> p c x", c=4)[:, :, 0:4], func=AF.Copy),
                          [BN[6]], ["ysT"])
                    S.add(ACT, lambda e: e.activation(out=ysT[64:128, :, b * 4:(b + 1) * 4],
                                                      in_=bank[6][64:128, 0:32].rearrange("p (c x) -> p c x", c=4)[:, :, 4:8], func=AF.Copy),
                          [BN[6]], ["ysT"])

                units = []
                for b in range(NSEQ):
                    for G in range(NG - 1, -1, -1):
                        units.append((b, G, len(units)))
                scores(units[0])
                for ui, u in enumerate(units):
                    if ui + 1 < len(units):
                        scores(units[ui + 1])
                    if u[1] == NG - 1:
                        new_block(u[0])
                    chain_av(u)
                    if u[1] == 0:
                        extract(u[0])
                S.add(DVE, lambda e: e.tensor_tensor(out=mixTs[:, 8:12, :], in0=ysT[:], in1=gate_s[:], op=ALU.mult), ["ysT", "gate_s"], ["mixTs"])

                for half in range(2):
                    S.add(POOL, lambda e, half=half: e.dma_start(
                        out=wout[:], in_=w_out[l][:, half * 512:(half + 1) * 512].rearrange("(q p) f -> p q f", p=128)),
                        writes=["wout"], dma=True, chan="w")
                    for f in range(16):
                        S.add(PE, lambda e, f=f, half=half: e.matmul(bank[half][0:16, :], lhsT=mixTs[:, f, :], rhs=wout[:, f, :], start=(f == 0),
                                                                    stop=(f == 15), skip_group_check=True), ["mixTs", "wout"], [BN[half]])
                    S.add(ACT, lambda e, half=half: e.activation(out=os_s[:, half * 512:(half + 1) * 512], in_=bank[half][0:16, :], func=AF.Copy),
                          [BN[half]], ["os_s"])
                S.add(ACT, lambda e: e.activation(out=sqj[0:16, :], in_=os_s[:], func=AF.Square, accum_out=stat[0:16, 2:3]), ["os_s"], ["sqj", "stat"])
                S.add(ACT, lambda e: e.activation(out=stat[0:16, 2:3], in_=stat[0:16, 2:3], func=AF.Ln, scale=1.0 / D, bias=epst[0:16, :]),
                      ["stat", "epst"], ["stat"])
                S.add(ACT, lambda e: e.activation(out=stat[0:16, 2:3], in_=stat[0:16, 2:3], func=AF.Exp, scale=-0.5), ["stat"], ["stat"])
                S.add(DVE, lambda e: e.scalar_tensor_tensor(out=os_s[:], in0=os_s[:], scalar=stat[0:16, 2:3], in1=gpost[0:16, :],
                                                            op0=ALU.mult, op1=ALU.mult), ["os_s", "stat", "gpost"], ["os_s"])
                pool_op(lambda e: e.tensor_tensor(out=xs_t[:], in0=xs_t[:], in1=os_s[:], op=ALU.add), ["xs_t", "os_s"], ["xs_t"])

            ring = [0, 0]
            rows_f = sb2("rows_f", [128, NSEQ * NP])
            for l in range(NL):
                sample_layer(l)
            S.add(SP, lambda e: e.dma_start(out=gmisc[:], in_=final_g[0:1, :].partition_broadcast(128)), writes=["gmisc"], dma=True, chan="ld")
            S.add(ACT, lambda e: e.activation(out=sqj[0:16, :], in_=xs_t[:], func=AF.Square, accum_out=stat[0:16, 0:1]), ["xs_t"], ["sqj", "stat"])
            S.add(ACT, lambda e: e.activation(out=stat[0:16, 0:1], in_=stat[0:16, 0:1], func=AF.Ln, scale=1.0 / D, bias=epst[0:16, :]),
                  ["stat", "epst"], ["stat"])
            S.add(ACT, lambda e: e.activation(out=stat[0:16, 0:1], in_=stat[0:16, 0:1], func=AF.Exp, scale=-0.5), ["stat"], ["stat"])
            S.add(DVE, lambda e: e.scalar_tensor_tensor(out=os_s[:], in0=xs_t[:], scalar=stat[0:16, 0:1], in1=gmisc[0:16, :],
                                                        op0=ALU.mult, op1=ALU.mult), ["xs_t", "stat", "gmisc", "os_s"], ["os_s"])
            S.add(SP, lambda e: e.dma_start(out=y_s, in_=os_s[:]), ["os_s"], dma=True, chan="st")
          S.barrier()

        if do_prompt:
            xt = sb("xt", [128, 4, D])
            x1 = nc.dram_tensor("x1", [SEQ, D], F32).ap()
            kT_all = sb("kT_all", [128, 4, SEQ], BF16)
            v_all = sb("v_all", [128, 16, 512], BF16)
            hT = sb("hT", [128, 8, 512], BF16)
            mixT = sb("mixT", [128, 16, 512], BF16)
            gate = sb("gate", [128, 4, 512], BF16)
            u_ext = sb("u_ext", [128, 4, 542])
            yflat = sb("yconv", [128, 2048])
            yconv = yflat[:].rearrange("p (c t) -> p c t", c=4)
            sconv = sb("sconv", [128, 4, 512], BF16)
            stats = sb("stats", [128, 2, 512])
            pu_ext = sb("pu_ext", [128, 4, 527])
            ptmp = [yflat[:, 0:527], yflat[:, 1024:1551]]
            dT = sconv
            corr = sb("corr", [128, 4, 16])
            qT = sb("qT", [128, 4, 512], BF16)
            esb = [sb("esb%d" % i, [128, 512]) for i in range(2)]
            spb = [sb("spb%d" % i, [128, 512], BF16) for i in range(2)]
            aTb = [sb("aTb%d" % i, [128, 512], BF16) for i in range(2)]
            spacc = sb("spacc", [128, 512])
            spaccb = sb("spaccb", [128, 512], BF16)
            mkT = sb("mkT", [128, 4, NMEM], BF16)
            mvb = sb("mvb", [128, 2, 512], BF16)
            memhT = hT[:, :, 0:NMEM]
            rden = esb[0]
            gtmp = esb[1]
            outst = kvst[0]

            for g, w in enumerate(POOLW):
                pool_op(lambda e, g=g, w=w: e.memset(corr[:, g, :], 1.0 / w), writes=["corr"])
                for t in range(w - 1):
                    pool_op(lambda e, g=g, t=t: e.memset(corr[:, g, t:t + 1], 1.0 / (t + 1)), ["corr"], ["corr"])


            def mem_kv_layer(l):
                S.add(SP, lambda e: e.dma_start(out=gmisc[:], in_=mem_g[l:l + 1, :].partition_broadcast(128)),
                      writes=["gmisc"], dma=True, chan="ld")
                for mc in range(2):
                    xm = kvst[0]
                    xm2 = kvst[1]
                    S.add(SP, lambda e, mc=mc: e.dma_start(out=xm[:], in_=memp[mc * 128:(mc + 1) * 128, 0:512]),
                          writes=["kvst0"], dma=True, chan="ld")
                    S.add(SP, lambda e, mc=mc: e.dma_start(out=xm2[:], in_=memp[mc * 128:(mc + 1) * 128, 512:1024]),
                          writes=["kvst1"], dma=True, chan="ld")
                    S.add(ACT, lambda e: e.activation(out=sqj[:, 0:512], in_=xm[:], func=AF.Square,
                                                      accum_out=stat[:, 0:1]), ["kvst0"], ["sqj", "stat"])
                    S.add(ACT, lambda e: e.activation(out=sqj[:, 512:1024], in_=xm2[:], func=AF.Square,
                                                      accum_out=stat[:, 1:2]), ["kvst1"], ["sqj", "stat"])
                    S.add(DVE, lambda e: e.tensor_tensor(out=stat[:, 0:1], in0=stat[:, 0:1], in1=stat[:, 1:2], op=ALU.add),
                          ["stat"], ["stat"])
                    S.add(ACT, lambda e: e.activation(out=stat[:, 0:1], in_=stat[:, 0:1], func=AF.Ln,
                                                      scale=1.0 / D, bias=epst[:]), ["stat", "epst"], ["stat"])
                    S.add(ACT, lambda e: e.activation(out=stat[:, 0:1], in_=stat[:, 0:1], func=AF.Exp, scale=-0.5),
                          ["stat"], ["stat"])
                    h = hs[mc % 2]
                    hn = "hs%d" % (mc % 2)
                    S.add(DVE, lambda e, h=h: e.scalar_tensor_tensor(out=h[:, 0:512], in0=xm[:], scalar=stat[:, 0:1],
                                                                      in1=gmisc[:, 0:512], op0=ALU.mult, op1=ALU.mult),
                          ["kvst0", "stat", "gmisc"], [hn])
                    S.add(DVE, lambda e, h=h: e.scalar_tensor_tensor(out=h[:, 512:1024], in0=xm2[:], scalar=stat[:, 0:1],
                                                                      in1=gmisc[:, 512:1024], op0=ALU.mult, op1=ALU.mult),
                          ["kvst1", "stat", "gmisc"], [hn])
                    for kc in range(8):
                        S.add(PE, lambda e, h=h, kc=kc: e.transpose(out=bankT[:, kc * 128:(kc + 1) * 128],
                                                                     in_=h[:, kc * 128:(kc + 1) * 128], identity=identb[:]),
                              [hn, "identb"], [BT])
                    S.add(ACT, lambda e, mc=mc: e.activation(out=memhT[:, :, mc * 128:(mc + 1) * 128],
                                                            in_=bankT[:].rearrange("p (k t) -> p k t", k=8), func=AF.Copy),
                          [BT], ["hT"])
                for half in range(2):
                    i = wctr[0] % 2
                    wctr[0] += 1
                    S.add(POOL, lambda e, half=half, i=i: e.dma_start(
                        out=win[i][:], in_=w_mem[l][:, half * 512:(half + 1) * 512].rearrange("(q p) f -> p q f", p=128)),
                        writes=["win%d" % i], dma=True, chan="w")
                    wn = "win%d" % i
                    for mc in range(2):
                        b = mc % 2
                        for kc in range(8):
                            S.add(PE, lambda e, i=i, mc=mc, kc=kc, b=b: e.matmul(
                                bank[b][:], lhsT=memhT[:, kc, mc * 128:(mc + 1) * 128], rhs=win[i][:, kc, :],
                                start=(kc == 0), stop=(kc == 7), skip_group_check=True),
                                ["hT", wn], [BN[b]])
                        kv = kvst[b]
                        S.add(ACT, lambda e, kv=kv, b=b: e.activation(out=kv[:], in_=bank[b][:], func=AF.Copy),
                              [BN[b]], ["kvst%d" % b])
                        dst = (mk_p if half == 0 else mv_p)
                        S.add(SP, lambda e, kv=kv, dst=dst, mc=mc: e.dma_start(out=dst[l, mc * 128:(mc + 1) * 128, :], in_=kv[:]),
                              ["kvst%d" % b], dma=True, chan="st")
                        if half == 1:
                            S.add(DVE, lambda e, kv=kv, mc=mc: e.tensor_copy(out=mvb[:, mc, :], in_=kv[:]),
                                  ["kvst%d" % b], ["mvb"])
                    if half == 0:
                        for hh in range(4):
                            b = hh % 2
                            for kc in range(8):
                                S.add(PE, lambda e, i=i, hh=hh, kc=kc, b=b: e.matmul(
                                    bank[b][:, 0:NMEM], lhsT=win[i][:, kc, hh * 128:(hh + 1) * 128], rhs=memhT[:, kc, :],
                                    start=(kc == 0), stop=(kc == 7), skip_group_check=True),
                                    ["hT", wn], [BN[b]])
                            S.add(ACT, lambda e, hh=hh, b=b: e.activation(out=mkT[:, hh, :], in_=bank[b][:, 0:NMEM], func=AF.Copy),
                                  [BN[b]], ["mkT"])

            def prompt_tile(l, q):
                t0 = q * 512
                last = (q == 3)
                for i in range(4):
                    ti = 4 * q + i
                    src = xp if l == 0 else x1
                    S.add(SP, lambda e, i=i, ti=ti, src=src: e.dma_start(out=xt[:, i, :], in_=src[ti * 128:(ti + 1) * 128, :]),
                          reads=["x1_%d" % ti], writes=["xt%d" % i], dma=True, chan="ld")
                for i in range(4):
                    S.add(ACT, lambda e, i=i: e.activation(out=sqj[:], in_=xt[:, i, :], func=AF.Square,
                                                            accum_out=stat[:, 4 + i:5 + i]), ["xt%d" % i], ["sqj", "statp%d" % i])
                for i in range(4):
                    S.add(ACT, lambda e, i=i: e.activation(out=stat[:, 4 + i:5 + i], in_=stat[:, 4 + i:5 + i], func=AF.Ln,
                                                          scale=1.0 / D, bias=epst[:]), ["statp%d" % i, "epst"], ["statp%d" % i])
                for i in range(4):
                    S.add(ACT, lambda e, i=i: e.activation(out=stat[:, 4 + i:5 + i], in_=stat[:, 4 + i:5 + i], func=AF.Exp, scale=-0.5),
                          ["statp%d" % i], ["statp%d" % i])
                for i in range(4):
                    xn = "xt%d" % i
                    h = hs[i % 2]
                    hn = "hs%d" % (i % 2)
                    S.add(DVE, lambda e, h=h, i=i: e.scalar_tensor_tensor(out=h[:], in0=xt[:, i, :], scalar=stat[:, 4 + i:5 + i],
                                                                            in1=gpre[:], op0=ALU.mult, op1=ALU.mult),
                          [xn, "statp%d" % i, "gpre"], [hn])
                    for kc in range(8):
                        S.add(PE, lambda e, h=h, kc=kc: e.transpose(out=bankT[:, kc * 128:(kc + 1) * 128],
                                                                     in_=h[:, kc * 128:(kc + 1) * 128], identity=identb[:]),
                              [hn, "identb"], [BT])
                    S.add(ACT, lambda e, i=i: e.activation(out=hT[:, :, i * 128:(i + 1) * 128],
                                                          in_=bankT[:].rearrange("p (k t) -> p k t", k=8), func=AF.Copy),
                          [BT], ["hT"])

                def proj_group(col0, evac):
                    i = load_win(l, col0)
                    wn = "win%d" % i
                    for ch in range(4):
                        b = ch
                        for kc in range(8):
                            S.add(PE, lambda e, i=i, ch=ch, kc=kc, b=b: e.matmul(
                                bank[b][:], lhsT=win[i][:, kc, ch * 128:(ch + 1) * 128], rhs=hT[:, kc, :],
                                start=(kc == 0), stop=(kc == 7), skip_group_check=True),
                                ["hT", wn], [BN[b]])
                        evac(ch, b)
                    return i

                def proj_tok(i, name, dst_dram, keep_bf=None):
                    wn = "win%d" % i
                    for sub in range(4):
                        b = sub % 2
                        for kc in range(8):
                            S.add(PE, lambda e, sub=sub, kc=kc, b=b: e.matmul(
                                bank[b][:], lhsT=hT[:, kc, sub * 128:(sub + 1) * 128], rhs=win[i][:, kc, :],
                                start=(kc == 0), stop=(kc == 7), skip_group_check=True),
                                ["hT", wn], [BN[b]])
                        kv = kvst[b]
                        S.add(ACT, lambda e, kv=kv, b=b: e.activation(out=kv[:], in_=bank[b][:], func=AF.Copy),
                              [BN[b]], ["kvst%d" % b])
                        S.add(SP, lambda e, kv=kv, sub=sub: e.dma_start(
                            out=dst_dram[l, t0 + sub * 128:t0 + (sub + 1) * 128, :], in_=kv[:]),
                            ["kvst%d" % b], dma=True, chan="st")
                        if keep_bf is not None:
                            S.add(ACT, lambda e, kv=kv, sub=sub: e.activation(out=keep_bf[:, 4 * q + sub, :], in_=kv[:], func=AF.Copy),
                                  ["kvst%d" % b], ["v_all"])

                def br_c():
                    if q == 0:
                        pool_op(lambda e: e.memset(u_ext[:, :, 0:30], 0.0), ["u_ext"], ["u_ext"])
                    else:
                        for ch in range(4):
                            pool_op(lambda e, ch=ch: e.tensor_copy(out=u_ext[:, ch, 0:30], in_=u_ext[:, ch, 512:542]),
                                    ["u_ext"], ["u_ext"])
                    proj_group(512, lambda ch, b: S.add(
                        ACT, lambda e: e.activation(out=u_ext[:, ch, 30:542], in_=bank[b][:], func=AF.Sigmoid),
                        [BN[b]], ["u_ext"]))
                    proj_group(0, lambda ch, b: S.add(
                        DVE, lambda e: e.tensor_tensor(out=u_ext[:, ch, 30:542], in0=bank[b][:], in1=u_ext[:, ch, 30:542], op=ALU.mult),
                        [BN[b], "u_ext"], ["u_ext"]))
                    proj_group(1024, lambda ch, b: S.add(
                        ACT, lambda e: e.activation(out=mixT[:, ch, :], in_=bank[b][:], func=AF.Silu), [BN[b]], ["mixT%d" % ch]))
                    for k in range(0, 31):
                        for ch in range(4):
                            yn = "yconv%d" % ch
                            if k == 0:
                                S.add(DVE, lambda e, ch=ch: e.tensor_scalar(out=yconv[:, ch, :], in0=u_ext[:, ch, 0:512],
                                                                           scalar1=ctab[:, ch, 0:1], scalar2=ctab[:, ch, 31:32],
                                                                           op0=ALU.mult, op1=ALU.add), ["u_ext", "ctab"], [yn])
                            else:
                                S.add(DVE, lambda e, ch=ch, k=k: e.scalar_tensor_tensor(
                                    out=yconv[:, ch, :], in0=u_ext[:, ch, k:k + 512], scalar=ctab[:, ch, k:k + 1],
                                    in1=yconv[:, ch, :], op0=ALU.mult, op1=ALU.add), ["u_ext", "ctab", yn], [yn])
                def br_c_tail():
                    for ch in range(4):
                        yn = "yconv%d" % ch
                        S.add(ACT, lambda e, ch=ch: e.activation(out=aTb[0][:], in_=yconv[:, ch, :], func=AF.Copy),
                              [yn], ["aTb0"])
                        S.add(ACT, lambda e, ch=ch: e.activation(out=aTb[1][:], in_=yconv[:, ch, :], func=AF.Square),
                              [yn], ["aTb1"])
                        S.add(PE, lambda e, ch=ch: e.matmul(bank[0][:], lhsT=inv512[:], rhs=aTb[0][:], start=(ch == 0),
                                                            stop=(ch == 3), skip_group_check=True), ["aTb0", "inv512"], [BN[0]])
                        S.add(PE, lambda e, ch=ch: e.matmul(bank[1][:], lhsT=inv512[:], rhs=aTb[1][:], start=(ch == 0),
                                                            stop=(ch == 3), skip_group_check=True), ["aTb1", "inv512"], [BN[1]])
                    S.add(ACT, lambda e: e.activation(out=stats[:, 0, :], in_=bank[0][:], func=AF.Copy), [BN[0]], ["stats0"])
                    S.add(ACT, lambda e: e.activation(out=stats[:, 1, :], in_=bank[0][:], func=AF.Square), [BN[0]], ["stats1"])
                    S.add(DVE, lambda e: e.tensor_tensor(out=stats[:, 1, :], in0=bank[1][:], in1=stats[:, 1, :], op=ALU.subtract),
                          [BN[1], "stats1"], ["stats1"])
                    S.add(ACT, lambda e: e.activation(out=stats[:, 1, :], in_=stats[:, 1, :], func=AF.Ln, bias=epst[:]),
                          ["stats1", "epst"], ["stats1"])
                    S.add(ACT, lambda e: e.activation(out=stats[:, 1, :], in_=stats[:, 1, :], func=AF.Exp, scale=-0.5),
                          ["stats1"], ["stats1"])
                    for ch in range(4):
                        yn = "yconv%d" % ch
                        eng = DVE
                        S.add(eng, lambda e, ch=ch: e.tensor_tensor(out=yconv[:, ch, :], in0=yconv[:, ch, :], in1=stats[:, 0, :],
                                                                   op=ALU.subtract), [yn, "stats0"], [yn])
                        S.add(eng, lambda e, ch=ch: e.tensor_tensor(out=yconv[:, ch, :], in0=yconv[:, ch, :], in1=stats[:, 1, :],
                                                                   op=ALU.mult), [yn, "stats1"], [yn])
                        S.add(ACT, lambda e, ch=ch: e.activation(out=sconv[:, ch, :], in_=yconv[:, ch, :], func=AF.Silu,
                                                                scale=ctab[:, ch, 32:33], bias=ctab[:, ch, 33:34]),
                              [yn, "ctab"], ["sconv"])
                    for co in range(4):
                        b = co % 2
                        for ch in range(4):
                            S.add(PE, lambda e, co=co, ch=ch, b=b: e.matmul(
                                bank[b][:], lhsT=wpw[:, ch, co * 128:(co + 1) * 128], rhs=sconv[:, ch, :],
                                start=(ch == 0), stop=(ch == 3), skip_group_check=True), ["sconv", "wpw"], [BN[b]])
                        S.add(DVE, lambda e, co=co, b=b: e.scalar_tensor_tensor(
                            out=mixT[:, co, :], in0=bank[b][:], scalar=ctab[:, co, 34:35], in1=mixT[:, co, :],
                            op0=ALU.add, op1=ALU.mult), [BN[b], "ctab", "mixT%d" % co], ["mixT%d" % co])
                    if last:
                        for ch in range(4):
                            S.add(PE, lambda e, ch=ch: e.transpose(out=bank[0][0:30, ch * 128:(ch + 1) * 128],
                                                                   in_=u_ext[:, ch, 512:542], identity=identf[:]),
                                  ["u_ext", "identf"], [BN[0]])
                        S.add(ACT, lambda e: e.activation(out=outst[0:30, :], in_=bank[0][0:30, :], func=AF.Copy), [BN[0]], ["kvst0"])
                        S.add(SP, lambda e: e.dma_start(out=conv_p[l], in_=outst[0:30, :]), ["kvst0"], dma=True, chan="st")

                def br_p():
                    if q == 0:
                        pool_op(lambda e: e.memset(pu_ext[:, :, 0:15], 0.0), ["pu_ext"], ["pu_ext"])
                    else:
                        for ch in range(4):
                            pool_op(lambda e, ch=ch: e.tensor_copy(out=pu_ext[:, ch, 0:15], in_=pu_ext[:, ch, 512:527]),
                                    ["pu_ext"], ["pu_ext"])
                    proj_group(1536, lambda ch, b: S.add(
                        ACT, lambda e: e.activation(out=pu_ext[:, ch, 15:527], in_=bank[b][:], func=AF.Copy), [BN[b]], ["pu_ext"]))
                    proj_group(2048, lambda ch, b: S.add(
                        ACT, lambda e: e.activation(out=mixT[:, 4 + ch, :], in_=bank[b][:], func=AF.Silu), [BN[b]], ["mixT%d" % (4 + ch)]))
                def br_p_tail():
                    for g, w in enumerate(POOLW):
                        cur = pu_ext[:, g, :]
                        curn = ("pu_ext",)
                        lo = 0
                        d = 1
                        k = 0
                        while d < w:
                            nxt = ptmp[k % 2]
                            nn = ("yconv0", "yconv1") if k % 2 == 0 else ("yconv2", "yconv3")
                            S.add(DVE, lambda e, cur=cur, nxt=nxt, lo=lo, d=d: e.tensor_tensor(
                                out=nxt[:, lo + d:527], in0=cur[:, lo + d:527], in1=cur[:, lo:527 - d], op=ALU.add),
                                list(curn), list(nn))
                            cur, curn = nxt, nn
                            lo += d
                            d *= 2
                            k += 1
                        S.add(DVE, lambda e, cur=cur, g=g, w=w: e.scalar_tensor_tensor(
                            out=dT[:, g, :], in0=cur[:, 15:527], scalar=1.0 / w, in1=pu_ext[:, g, 15:527],
                            op0=ALU.mult, op1=ALU.subtract), list(curn) + ["pu_ext"], ["sconv"])
                        if q == 0:
                            S.add(DVE, lambda e, cur=cur, g=g: e.tensor_tensor(
                                out=cur[:, 15:31], in0=cur[:, 15:31], in1=corr[:, g, :], op=ALU.mult), list(curn) + ["corr", "sconv"], list(curn))
                            S.add(DVE, lambda e, cur=cur, g=g: e.tensor_tensor(
                                out=dT[:, g, 0:16], in0=cur[:, 15:31], in1=pu_ext[:, g, 15:31], op=ALU.subtract),
                                list(curn) + ["pu_ext"], ["sconv"])
                        b = g % 2
                        S.add(PE, lambda e, g=g, b=b: e.matmul(bank[b][:], lhsT=wpool[:, g, :], rhs=dT[:, g, :], start=True, stop=True,
                                                               skip_group_check=True), ["sconv", "wpool"], [BN[b]])
                        S.add(DVE, lambda e, g=g, b=b: e.scalar_tensor_tensor(
                            out=mixT[:, 4 + g, :], in0=bank[b][:], scalar=ctab[:, g, 35:36], in1=mixT[:, 4 + g, :],
                            op0=ALU.mult, op1=ALU.mult), [BN[b], "ctab", "mixT%d" % (4 + g)], ["mixT%d" % (4 + g)])
                    if last:
                        for ch in range(4):
                            S.add(PE, lambda e, ch=ch: e.transpose(out=bank[0][0:15, ch * 128:(ch + 1) * 128],
                                                                   in_=pu_ext[:, ch, 512:527], identity=identf[:]),
                                  ["pu_ext", "identf"], [BN[0]])
                        S.add(ACT, lambda e: e.activation(out=outst[0:15, :], in_=bank[0][0:15, :], func=AF.Copy), [BN[0]], ["kvst0"])
                        S.add(SP, lambda e: e.dma_start(out=pool_p[l], in_=outst[0:15, :]), ["kvst0"], dma=True, chan="st")

                def br_s():
                    proj_group(2560, lambda ch, b: S.add(
                        ACT, lambda e: e.activation(out=qT[:, ch, :], in_=bank[b][:], func=AF.Copy, scale=0.125), [BN[b]], ["qT"]))
                    iw = proj_group(3072, lambda ch, b: S.add(
                        ACT, lambda e: e.activation(out=kT_all[:, ch, t0:t0 + 512], in_=bank[b][:], func=AF.Copy), [BN[b]], ["kT_all"]))
                    proj_tok(iw, "k", k_p)
                    iw = load_win(l, 3584)
                    proj_tok(iw, "v", v_p, keep_bf=v_all)
                    proj_group(4096, lambda ch, b: S.add(
                        ACT, lambda e: e.activation(out=gate[:, ch, :], in_=bank[b][:], func=AF.Silu), [BN[b]], ["gate"]))
                def br_s_attn():
                    S.add(POOL, lambda e: e.dma_start(out=wout[:], in_=w_out[l][:, 0:512].rearrange("(q p) f -> p q f", p=128)),
                          writes=["wout"], dma=True, chan="w")
                    steps = [(h, kb) for h in range(8) for kb in range(4 * q + 3, -1, -1)]

                    def geom(n):
                        h, kb = steps[n]
                        hp, hl = h // 2, h % 2
                        pr = slice(hl * 64, hl * 64 + 64)
                        dg = kb - 4 * q
                        c0 = 128 * dg if dg > 0 else 0
                        return h, kb, hp, pr, c0, (dg >= 0), n % 2

                    def stage1(n):
                        h, kb, hp, pr, c0, diag, nb = geom(n)
                        A, An = bank[2 + nb], BN[2 + nb]
                        e_t, e_n = esb[nb], "esb%d" % nb
                        sp_t, sp_n = spb[nb], "spb%d" % nb
                        ksl = slice(kb * 128, kb * 128 + 128)
                        S.add(PE, lambda e: e.matmul(A[:, c0:512], lhsT=kT_all[pr, hp, ksl], rhs=qT[pr, hp, c0:512], start=True, stop=True,
                                                     skip_group_check=True), ["kT_all", "qT"], [An])
                        S.add(ACT, lambda e: e.activation(out=e_t[:, c0:512], in_=A[:, c0:512], func=AF.Exp, bias=sbb_t[:, h:h + 1]),
                              [An, "sbb_t"], [e_n])

                    def stage1b(n):
                        h, kb, hp, pr, c0, diag, nb = geom(n)
                        e_t, e_n = esb[nb], "esb%d" % nb
                        sp_t, sp_n = spb[nb], "spb%d" % nb
                        S.add(ACT, lambda e: e.activation(out=sp_t[:, c0:512], in_=e_t[:, c0:512], func=AF.Ln, bias=1.0), [e_n], [sp_n])
                        if diag:
                            pool_op(lambda e: e.tensor_tensor(out=sp_t[:, c0:c0 + 128], in0=sp_t[:, c0:c0 + 128], in1=mask01[:], op=ALU.mult),
                                    [sp_n, "mask01"], [sp_n])

                    def stage2(n):
                        h, kb, hp, pr, c0, diag, nb = geom(n)
                        first = (kb == 4 * q + 3)
                        Bk, Bn = bank[4 + nb], BN[4 + nb]
                        sp_t, sp_n = spb[nb], "spb%d" % nb
                        a_t, a_n = aTb[nb], "aTb%d" % nb
                        ob = 6 if h % 2 == 0 else 1
                        ksl = slice(kb * 128, kb * 128 + 128)
                        if first:
                            pool_op(lambda e: e.memset(spacc[:], 0.0), ["spacc"], ["spacc"])
                            pool_op(lambda e: e.memset(spaccb[:], 0.0), ["spaccb"], ["spaccb"])
                        S.add(PE, lambda e: e.matmul(Bk[:, c0:512], lhsT=negU[:], rhs=sp_t[:, c0:512], start=True, stop=False,
                                                     skip_group_check=True), [sp_n, "negU"], [Bn])
                        if not first:
                            S.add(PE, lambda e: e.matmul(Bk[:, c0:512], lhsT=negOnes[:], rhs=spaccb[:, c0:512], start=False, stop=False,
                                                         skip_group_check=True), ["spaccb", "negOnes"], [Bn])
                        S.add(PE, lambda e: e.matmul(Bk[:, c0:512], lhsT=kT_all[pr, hp, ksl], rhs=qT[pr, hp, c0:512], start=False, stop=True,
                                                     skip_group_check=True), ["kT_all", "qT"], [Bn])
                        S.add(ACT, lambda e: e.activation(out=a_t[:, c0:512], in_=Bk[:, c0:512], func=AF.Exp, bias=sbb_t[:, h:h + 1]),
                              [Bn, "sbb_t"], [a_n])
                        if diag:
                            pool_op(lambda e: e.tensor_tensor(out=a_t[:, c0:c0 + 128], in0=a_t[:, c0:c0 + 128], in1=mask01[:], op=ALU.mult),
                                    [a_n, "mask01"], [a_n])
                        S.add(PE, lambda e: e.matmul(bank[ob][pr, c0:512], lhsT=v_all[:, kb, h * 64:(h + 1) * 64], rhs=a_t[:, c0:512],
                                                     start=first, stop=(kb == 0), skip_group_check=True), [a_n, "v_all"], [BN[ob]])
                        if kb > 0:
                            pool_op(lambda e: e.tensor_tensor(out=spacc[:, c0:512], in0=spacc[:, c0:512], in1=sp_t[:, c0:512], op=ALU.add),
                                    ["spacc", sp_n], ["spacc"])
                            S.add(DVE, lambda e: e.tensor_copy(out=spaccb[:, c0:512], in_=spacc[:, c0:512]), ["spacc", "spaccb"], ["spaccb"])
                        else:
                            S.add(DVE, lambda e: e.tensor_tensor(out=mixT[pr, 8 + hp, :], in0=bank[ob][pr, :], in1=gate[pr, hp, :], op=ALU.mult),
                                  [BN[ob], "gate"], ["mixT%d" % (8 + hp)])

                    stage1(0)
                    stage1b(0)
                    for n in range(len(steps)):
                        if n + 1 < len(steps):
                            stage1(n + 1)
                        stage2(n)
                        if n + 1 < len(steps):
                            stage1b(n + 1)

                def br_m():
                    proj_group(4608, lambda ch, b: S.add(
                        ACT, lambda e: e.activation(out=qT[:, ch, :], in_=bank[b][:], func=AF.Copy, scale=128 ** -0.5),
                        [BN[b]], ["qT"]))
                    proj_group(5120, lambda ch, b: S.add(
                        ACT, lambda e: e.activation(out=gate[:, ch, :], in_=bank[b][:], func=AF.Silu), [BN[b]], ["gate"]))
                    for hh in range(4):
                        for mc in range(2):
                            b = 2 + mc
                            S.add(PE, lambda e, hh=hh, mc=mc, b=b: e.matmul(
                                bank[b][:], lhsT=mkT[:, hh, mc * 128:(mc + 1) * 128], rhs=qT[:, hh, :], start=True, stop=True,
                                skip_group_check=True), ["mkT", "qT"], [BN[b]])
                            S.add(ACT, lambda e, mc=mc, b=b: e.activation(out=spb[mc][:], in_=bank[b][:], func=AF.Exp),
                                  [BN[b]], ["spb%d" % mc])
                        for mc in range(2):
                            S.add(PE, lambda e, hh=hh, mc=mc: e.matmul(
                                bank[4][:], lhsT=mvb[:, mc, hh * 128:(hh + 1) * 128], rhs=spb[mc][:], start=(mc == 0),
                                stop=(mc == 1), skip_group_check=True), ["mvb", "spb%d" % mc], [BN[4]])
                            S.add(PE, lambda e, mc=mc: e.matmul(
                                bank[5][:], lhsT=onesb[:], rhs=spb[mc][:], start=(mc == 0), stop=(mc == 1),
                                skip_group_check=True), ["onesb", "spb%d" % mc], [BN[5]])
                        S.add(DVE, lambda e: e.reciprocal(out=rden[:], in_=bank[5][:]), [BN[5]], ["esb0"])
                        S.add(DVE, lambda e, hh=hh: e.tensor_tensor(out=gtmp[:], in0=rden[:], in1=gate[:, hh, :], op=ALU.mult),
                              ["esb0", "gate"], ["esb1"])
                        S.add(DVE, lambda e, hh=hh: e.tensor_tensor(out=mixT[:, 12 + hh, :], in0=bank[4][:], in1=gtmp[:], op=ALU.mult),
                              [BN[4], "esb1"], ["mixT%d" % (12 + hh)])

                br_c()
                br_p()
                br_s()
                br_c_tail()
                br_p_tail()
                br_s_attn()
                br_m()
                if DEBUG and l == 0 and q == DEBUG_Q:
                    for f in range(16):
                        S.add(ACT, lambda e, f=f: e.activation(out=esb[f % 2][:], in_=mixT[:, f, :], func=AF.Copy),
                              ["mixT%d" % f], ["esb%d" % (f % 2)])
                        S.add(SP, lambda e, f=f: e.dma_start(out=dbg[:, f, :], in_=esb[f % 2][:]), ["esb%d" % (f % 2)],
                              dma=True, chan="st")
                for half in range(2):
                    if half == 1:
                        S.add(POOL, lambda e, half=half: e.dma_start(
                            out=wout[:], in_=w_out[l][:, half * 512:(half + 1) * 512].rearrange("(q p) f -> p q f", p=128)),
                            writes=["wout"], dma=True, chan="w")
                    for sub in range(4):
                        b = 2 * half + (sub % 2) if False else None
                    for sub in range(4):
                        bi = sub
                        pass
                    for sub in range(4):
                        b = (sub % 2) + 2 * half
                        for f in range(16):
                            S.add(PE, lambda e, sub=sub, f=f, b=b: e.matmul(
                                bank[b][:], lhsT=mixT[:, f, sub * 128:(sub + 1) * 128], rhs=wout[:, f, :],
                                start=(f == 0), stop=(f == 15), skip_group_check=True), ["mixT%d" % f, "wout"], [BN[b]])
                        S.add(ACT, lambda e, sub=sub, half=half, b=b: e.activation(
                            out=ostash[:, sub, half * 512:(half + 1) * 512], in_=bank[b][:], func=AF.Copy),
                            [BN[b]], ["ostash%d" % sub])
                for sub in range(4):
                    ti = 4 * q + sub
                    on = "ostash%d" % sub
                    S.add(ACT, lambda e, sub=sub: e.activation(out=sqj[:], in_=ostash[:, sub, :], func=AF.Square,
                                                              accum_out=stat[:, 2:3]), [on], ["sqj", "stat"])
                    S.add(ACT, lambda e: e.activation(out=stat[:, 2:3], in_=stat[:, 2:3], func=AF.Ln, scale=1.0 / D,
                                                      bias=epst[:]), ["stat", "epst"], ["stat"])
                    S.add(ACT, lambda e: e.activation(out=stat[:, 2:3], in_=stat[:, 2:3], func=AF.Exp, scale=-0.5),
                          ["stat"], ["stat"])
                    S.add(DVE, lambda e, sub=sub: e.scalar_tensor_tensor(
                        out=ostash[:, sub, :], in0=ostash[:, sub, :], scalar=stat[:, 2:3], in1=gpost[:],
                        op0=ALU.mult, op1=ALU.mult), [on, "stat", "gpost"], [on])
                    S.add(DVE, lambda e, sub=sub: e.tensor_tensor(
                        out=ostash[:, sub, :], in0=xt[:, sub, :], in1=ostash[:, sub, :], op=ALU.add), [on, "xt%d" % sub], [on])
                    if l < NL - 1:
                        S.add(SP, lambda e, sub=sub, ti=ti: e.dma_start(out=x1[ti * 128:(ti + 1) * 128, :], in_=ostash[:, sub, :]),
                              [on], ["x1_%d" % ti], dma=True, chan="st")
                    else:
                        S.add(ACT, lambda e, sub=sub: e.activation(out=sqj[:], in_=ostash[:, sub, :], func=AF.Square,
                                                                  accum_out=stat[:, 3:4]), [on], ["sqj", "stat"])
                        S.add(ACT, lambda e: e.activation(out=stat[:, 3:4], in_=stat[:, 3:4], func=AF.Ln, scale=1.0 / D,
                                                          bias=epst[:]), ["stat", "epst"], ["stat"])
                        S.add(ACT, lambda e: e.activation(out=stat[:, 3:4], in_=stat[:, 3:4], func=AF.Exp, scale=-0.5),
                              ["stat"], ["stat"])
                        S.add(DVE, lambda e, sub=sub: e.scalar_tensor_tensor(
                            out=ostash[:, sub, :], in0=ostash[:, sub, :], scalar=stat[:, 3:4], in1=gmisc[:],
                            op0=ALU.mult, op1=ALU.mult), [on, "stat", "gmisc"], [on])
                        S.add(SP, lambda e, sub=sub, ti=ti: e.dma_start(out=y_p[ti * 128:(ti + 1) * 128, :], in_=ostash[:, sub, :]),
                              [on], dma=True, chan="st")

            ostash = sb("ostash", [128, 4, D])

            for l in range(NL):
                load_layer_consts(l)
                mem_kv_layer(l)
                if l == NL - 1:
                    S.add(SP, lambda e: e.dma_start(out=gmisc[:], in_=final_g[0:1, :].partition_broadcast(128)),
                          writes=["gmisc"], dma=True, chan="ld")
                for q in range(4):
                    prompt_tile(l, q)
        S.emit()
    return nc


def prep_core(inp, c, NP, NPHYS, ckv=None):
    f = np.ascontiguousarray
    if ckv is None:
        ckv = np.concatenate([inp["cache_k"].reshape(DEPTH * NPHYS * 128, 512),
                              inp["cache_v"].reshape(DEPTH * NPHYS * 128, 512)], axis=1)
    cvec = np.concatenate([inp["conv_w_dw"], inp["conv_b_dw"][:, None, :], inp["conv_ln_g"][:, None, :],
                           inp["conv_ln_b"][:, None, :], inp["conv_b_pw"][:, None, :],
                           inp["pool_scale"][:, None, :]], axis=1)
    sl = slice(NSEQ * c, NSEQ * c + NSEQ)
    return {
        "xp": f(inp["x_prompt"][c]),
        "xs": f(inp["x_sample"][sl].reshape(NTOK, D)),
        "memp": f(inp["mem_prompt"][c]),
        "ckv": ckv,
        "cmk": f(inp["cache_mem_k"][:, sl].reshape(DEPTH, NSEQ, NMEM, 512)),
        "cmv": f(inp["cache_mem_v"][:, sl].reshape(DEPTH, NSEQ, NMEM, 512)),
        "stc": f(inp["state_conv"][:, sl]),
        "stp": f(inp["state_pool"][:, sl]),
        "ptab": f(inp["page_table"][sl].reshape(1, NSEQ * NP).astype(np.int32)),
        "pre_g": f(inp["pre_g"]), "post_g": f(inp["post_g"]), "mem_g": f(inp["mem_g"]),
        "final_g": f(inp["final_g"].reshape(1, D)),
        "w_in": inp["w_in"], "w_mem": inp["w_mem_kv"], "w_out": inp["w_out"],
        "w_pw": inp["conv_w_pw"], "pool_w": inp["pool_w"], "cvec": f(cvec), "sbb": f(inp["sb_bias"]),
    }


def assemble(results, ncores):
    r = results
    cat = lambda name: np.stack([r[c][name] for c in range(ncores)])
    y_p = cat("y_p")
    y_s = cat("y_s").reshape(ncores * NSEQ, 4, D)
    k_p = cat("k_p").transpose(1, 0, 2, 3).reshape(DEPTH, ncores, SEQ, 8, 64)
    v_p = cat("v_p").transpose(1, 0, 2, 3).reshape(DEPTH, ncores, SEQ, 8, 64)
    k_s = cat("k_s").transpose(1, 0, 2, 3).reshape(DEPTH, ncores * NSEQ, 4, 8, 64)
    v_s = cat("v_s").transpose(1, 0, 2, 3).reshape(DEPTH, ncores * NSEQ, 4, 8, 64)
    conv_p = cat("conv_p").transpose(1, 0, 2, 3)
    conv_s = cat("conv_s").transpose(1, 0, 2, 3, 4).reshape(DEPTH, ncores * NSEQ, 30, 512)
    pool_p = cat("pool_p").transpose(1, 0, 2, 3)
    pool_s = cat("pool_s").transpose(1, 0, 2, 3, 4).reshape(DEPTH, ncores * NSEQ, 15, 512)
    mk_p = cat("mk_p").transpose(1, 0, 2, 3).reshape(DEPTH, ncores, NMEM, 4, 128)
    mv_p = cat("mv_p").transpose(1, 0, 2, 3).reshape(DEPTH, ncores, NMEM, 4, 128)
    return tuple(np.ascontiguousarray(a, dtype=np.float32) for a in
                 (y_p, y_s, k_p, v_p, k_s, v_s, conv_p, conv_s, pool_p, pool_s, mk_p, mv_p))


def kernel(**inputs):
    inp = {k: np.asarray(v) for k, v in inputs.items()}
    NP = inp["page_table"].shape[1]
    NPHYS = inp["cache_k"].shape[1]
    nc = build(NP=NP, NPHYS=NPHYS)
    ckv = np.concatenate([inp["cache_k"].reshape(DEPTH * NPHYS * 128, 512),
                          inp["cache_v"].reshape(DEPTH * NPHYS * 128, 512)], axis=1)
    in_maps = [prep_core(inp, c, NP, NPHYS, ckv) for c in range(NCORE)]
    res = run_bass_kernel_spmd(nc, in_maps, core_ids=list(range(NCORE)))
    return assemble(res.results, NCORE)
```

```python
import contextlib
import numpy as np
import concourse.bass as bass
import concourse.mybir as mybir
from concourse.bass_utils import run_bass_kernel_spmd

F32 = mybir.dt.float32
BF16 = mybir.dt.bfloat16
I32 = mybir.dt.int32
AF = mybir.ActivationFunctionType
ALU = mybir.AluOpType
AX = mybir.AxisListType
PE, ACT, DVE, POOL, SP = "pe", "act", "dve", "pool", "sp"

D = 1024
SEQ = 2048
DEPTH = 2
DIN = 5632
NCORE = 8
NSEQ = 4
NTOK = 16
NMEM = 256
EPS = 1e-6
POOLW = (2, 4, 8, 16)
DEBUG = False
DEBUG_Q = 0
NL = 2
BRANCHES = 'cpsm'


class Op:
    __slots__ = ("id", "eng", "fn", "deps", "dma", "chan", "sig", "count")

    def __init__(self, id, eng, fn, dma, chan):
        self.id = id
        self.eng = eng
        self.fn = fn
        self.deps = set()
        self.dma = dma
        self.chan = chan
        self.sig = False
        self.count = 0


class Sched:
    def __init__(self, nc):
        self.nc = nc
        self.ops = []
        self.last_write = {}
        self.readers = {}
        self.barrier_id = None

    def add(self, eng, fn, reads=(), writes=(), dma=False, chan=None):
        op = Op(len(self.ops), eng, fn, dma, chan)
        for r in reads:
            lw = self.last_write.get(r)
            if lw is not None:
                op.deps.add(lw)
        for w in writes:
            lw = self.last_write.get(w)
            if lw is not None:
                op.deps.add(lw)
            for rd in self.readers.get(w, ()):
                op.deps.add(rd)
        if self.barrier_id is not None:
            op.deps.add(self.barrier_id)
        op.deps.discard(op.id)
        for r in reads:
            self.readers.setdefault(r, []).append(op.id)
        for w in writes:
            self.last_write[w] = op.id
            self.readers[w] = []
        self.ops.append(op)
        return op

    def barrier(self):
        last = {}
        dmas = set()
        for op in self.ops:
            if op.dma:
                dmas.add(op.id)
            else:
                last[op.eng] = op.id
        bop = Op(len(self.ops), POOL, lambda e: e.memset(self.bar_tile[0:1, 0:1], 0.0), False, None)
        bop.deps = set(last.values()) | dmas
        self.ops.append(bop)
        self.barrier_id = bop.id

    @staticmethod
    def _needs_wait(op, d):
        if d.dma or op.dma:
            return True
        if d.eng != op.eng:
            return True
        return d.eng != PE

    NSLOT = {"w": 6, "ld": 8, "st": 8}

    def emit(self):
        nc = self.nc
        ops = self.ops
        for op in ops:
            for d in op.deps:
                if self._needs_wait(op, ops[d]):
                    ops[d].sig = True
            if op.dma:
                op.sig = True
        cnt = {}
        nuse = {}
        for op in ops:
            if not op.sig:
                continue
            if op.dma:
                k = self.NSLOT.get(op.chan, 16)
                u = nuse.get(op.chan, 0)
                nuse[op.chan] = u + 1
                key = ("c", op.chan, u % k)
            else:
                key = ("e", op.eng, 0)
            cnt[key] = cnt.get(key, 0) + (16 if op.dma else 1)
            op.count = cnt[key]
            op.chan = key if op.dma else op.chan
        with contextlib.ExitStack() as st:
            sems = {}
            for key in cnt:
                sems[key] = st.enter_context(nc.semaphore("s_%s_%s_%d" % key))
            block = st.enter_context(nc.Block())
            engs = {PE: block.tensor, ACT: block.scalar, DVE: block.vector,
                    POOL: block.gpsimd, SP: block.sync}
            per_eng = {e: [] for e in engs}
            for op in ops:
                per_eng[op.eng].append(op)

            def keyof(o):
                return o.chan if o.dma else ("e", o.eng, 0)

            def make(e):
                def body(eng):
                    waited = {}
                    for op in per_eng[e]:
                        need = {}
                        for d in op.deps:
                            dop = ops[d]
                            if not self._needs_wait(op, dop):
                                continue
                            key = keyof(dop)
                            if dop.count > need.get(key, 0):
                                need[key] = dop.count
                        if op.dma and op.count > 16:
                            key = keyof(op)
                            need[key] = max(need.get(key, 0), op.count - 16)
                        for key, v in need.items():
                            if waited.get(key, 0) >= v:
                                continue
                            eng.wait_ge(sems[key], v)
                            waited[key] = v
                        ins = op.fn(eng)
                        if op.sig:
                            ins.then_inc(sems[keyof(op)], 16 if op.dma else 1)
                    if e == SP:
                        for key, v in cnt.items():
                            if key[0] == "c" and waited.get(key, 0) < v:
                                eng.wait_ge(sems[key], v)
                return body

            for e, dec in engs.items():
                dec(make(e))


def build(NP=64, NPHYS=2560, do_prompt=True, do_sample=True):
    nc = bass.Bass("TRN2", target_bir_lowering=False)
    S = Sched(nc)

    def din(name, shape, dt=F32):
        return nc.dram_tensor(name, list(shape), dt, kind="ExternalInput").ap()

    def dout(name, shape, dt=F32):
        return nc.dram_tensor(name, list(shape), dt, kind="ExternalOutput").ap()

    xp = din("xp", [SEQ, D])
    xs = din("xs", [NTOK, D])
    memp = din("memp", [NMEM, D])
    ckv = din("ckv", [DEPTH * NPHYS * 128, 1024])
    cmk = din("cmk", [DEPTH, NSEQ, NMEM, 512])
    cmv = din("cmv", [DEPTH, NSEQ, NMEM, 512])
    stc = din("stc", [DEPTH, NSEQ, 30, 512])
    stp = din("stp", [DEPTH, NSEQ, 15, 512])
    ptab = din("ptab", [1, NSEQ * NP], I32)
    pre_g = din("pre_g", [DEPTH, D])
    post_g = din("post_g", [DEPTH, D])
    mem_g = din("mem_g", [DEPTH, D])
    final_g = din("final_g", [1, D])
    w_in = din("w_in", [DEPTH, D, DIN])
    w_mem = din("w_mem", [DEPTH, D, 1024])
    w_out = din("w_out", [DEPTH, 2048, D])
    w_pw = din("w_pw", [DEPTH, 512, 512])
    pool_w = din("pool_w", [DEPTH, 4, 128, 128])
    cvec = din("cvec", [DEPTH, 36, 512])
    sbb = din("sbb", [DEPTH, 8])

    y_p = dout("y_p", [SEQ, D])
    y_s = dout("y_s", [NTOK, D])
    k_p = dout("k_p", [DEPTH, SEQ, 512])
    v_p = dout("v_p", [DEPTH, SEQ, 512])
    k_s = dout("k_s", [DEPTH, NTOK, 512])
    v_s = dout("v_s", [DEPTH, NTOK, 512])
    conv_p = dout("conv_p", [DEPTH, 30, 512])
    conv_s = dout("conv_s", [DEPTH, NSEQ, 30, 512])
    pool_p = dout("pool_p", [DEPTH, 15, 512])
    pool_s = dout("pool_s", [DEPTH, NSEQ, 15, 512])
    mk_p = dout("mk_p", [DEPTH, NMEM, 512])
    mv_p = dout("mv_p", [DEPTH, NMEM, 512])

    dbg = dout("dbg", [128, 16, 512]) if DEBUG else None
    st = contextlib.ExitStack()

    def sb(name, shape, dt=F32):
        return st.enter_context(nc.sbuf_tensor(name, list(shape), dt))

    def ps(name, shape, dt=F32):
        return st.enter_context(nc.psum_tensor(name, list(shape), dt))

    uid = [0]

    def U(prefix):
        uid[0] += 1
        return "%s_%d" % (prefix, uid[0])

    with st:
        bank = [ps("bank%d" % i, [128, 512], F32) for i in range(7)]
        bankT = ps("bankT", [128, 1024], BF16)
        BN = ["b%d" % i for i in range(7)]
        BT = "bT"

        identf = sb("identf", [128, 128])
        identb = sb("identb", [128, 128], BF16)
        negU = sb("negU", [128, 128], BF16)
        negOnes = sb("negOnes", [128, 128], BF16)
        onesb = sb("onesb", [128, 128], BF16)
        inv512 = sb("inv512", [128, 128], BF16)
        mask01 = sb("mask01", [128, 128], BF16)
        epst = sb("epst", [128, 1])
        S.bar_tile = sb("bar_tile", [128, 1])
        tmpc = sb("tmpc", [128, 128])

        def pool_op(fn, reads=(), writes=()):
            S.add(POOL, fn, reads, writes)

        pool_op(lambda e: e.memset(identf[:], 1.0), writes=["identf"])
        pool_op(lambda e: e.affine_select(out=identf[:], in_=identf[:], pattern=[[-1, 128]],
                                          compare_op=ALU.is_equal, fill=0.0, base=0,
                                          channel_multiplier=1), ["identf"], ["identf"])
        S.add(DVE, lambda e: e.tensor_copy(out=identb[:], in_=identf[:]), ["identf"], ["identb"])
        pool_op(lambda e: e.memset(tmpc[:], -1.0), writes=["tmpc"])
        pool_op(lambda e: e.affine_select(out=tmpc[:], in_=tmpc[:], pattern=[[-1, 128]],
                                          compare_op=ALU.is_ge, fill=0.0, base=0,
                                          channel_multiplier=1), ["tmpc"], ["tmpc"])
        S.add(DVE, lambda e: e.tensor_copy(out=negU[:], in_=tmpc[:]), ["tmpc"], ["negU"])
        pool_op(lambda e: e.memset(tmpc[:], 1.0), ["tmpc"], ["tmpc"])
        pool_op(lambda e: e.affine_select(out=tmpc[:], in_=tmpc[:], pattern=[[1, 128]],
                                          compare_op=ALU.is_ge, fill=0.0, base=-1,
                                          channel_multiplier=-1), ["tmpc"], ["tmpc"])
        S.add(DVE, lambda e: e.tensor_copy(out=mask01[:], in_=tmpc[:]), ["tmpc"], ["mask01"])
        pool_op(lambda e: e.memset(negOnes[:], -1.0), writes=["negOnes"])
        pool_op(lambda e: e.memset(onesb[:], 1.0), writes=["onesb"])
        pool_op(lambda e: e.memset(inv512[:], 1.0 / 512), writes=["inv512"])
        pool_op(lambda e: e.memset(epst[:], EPS), writes=["epst"])

        gpre = sb("gpre", [128, D])
        gpost = sb("gpost", [128, D])
        gmisc = sb("gmisc", [128, D])
        ctab = sb("ctab", [128, 4, 36])
        kvst = [sb("kvst%d" % i, [128, 512]) for i in range(2)]
        cv_sb = kvst[1][0:36, :]
        sbb_t = sb("sbb_t", [128, 8])
        wpw = sb("wpw", [128, 4, 512], BF16)
        wpool = sb("wpool", [128, 4, 128], BF16)
        win = [sb("win%d" % i, [128, 8, 512], BF16) for i in range(2)]
        wout = sb("wout", [128, 16, 512], BF16)
        stat = sb("stat", [128, 16])
        hs = [sb("hs%d" % i, [128, D], BF16) for i in range(2)]
        sqj = sb("sqj", [128, D], BF16)

        def load_layer_consts(l):
            S.add(SP, lambda e: e.dma_start(out=gpre[:], in_=pre_g[l:l + 1, :].partition_broadcast(128)),
                  writes=["gpre"], dma=True, chan="ld")
            S.add(SP, lambda e: e.dma_start(out=gpost[:], in_=post_g[l:l + 1, :].partition_broadcast(128)),
                  writes=["gpost"], dma=True, chan="ld")
            S.add(SP, lambda e: e.dma_start(out=sbb_t[:], in_=sbb[l:l + 1, :].partition_broadcast(128)),
                  writes=["sbb_t"], dma=True, chan="ld")
            S.add(SP, lambda e: e.dma_start(out=cv_sb[:], in_=cvec[l]), writes=["kvst1"], dma=True, chan="ld")
            for ch in range(4):
                b = BN[ch % 2]
                S.add(PE, lambda e, ch=ch: e.transpose(out=bank[ch % 2][:, 0:36], in_=cv_sb[:, ch * 128:(ch + 1) * 128],
                                                       identity=identf[0:36, 0:36]),
                      ["kvst1", "identf"], [b])
                S.add(DVE, lambda e, ch=ch: e.tensor_copy(out=ctab[:, ch, :], in_=bank[ch % 2][:, 0:36]), [b], ["ctab"])
            S.add(POOL, lambda e: e.dma_start(out=wpw[:], in_=w_pw[l].rearrange("(q p) f -> p q f", p=128)),
                  writes=["wpw"], dma=True, chan="w")
            S.add(POOL, lambda e: e.dma_start(out=wpool[:], in_=pool_w[l].rearrange("g c d -> c g d")),
                  writes=["wpool"], dma=True, chan="w")

        wctr = [0]

        def load_win(l, col0, ncols=512):
            i = wctr[0] % 2
            wctr[0] += 1
            S.add(POOL, lambda e: e.dma_start(out=win[i][:, :, 0:ncols],
                                              in_=w_in[l][:, col0:col0 + ncols].rearrange("(q p) f -> p q f", p=128)),
                  writes=["win%d" % i], dma=True, chan="w")
            return i

        def rstd_of(x_ap, npart, scale_n, col):
            S.add(ACT, lambda e: e.activation(out=sqj[0:npart, 0:x_ap.shape[1]], in_=x_ap, func=AF.Square,
                                              accum_out=stat[0:npart, col:col + 1]),
                  ["xres"], ["sqj", "stat"])
            S.add(ACT, lambda e: e.activation(out=stat[0:npart, col:col + 1], in_=stat[0:npart, col:col + 1],
                                              func=AF.Ln, scale=1.0 / scale_n, bias=epst[0:npart, :]),
                  ["stat", "epst"], ["stat"])
            S.add(ACT, lambda e: e.activation(out=stat[0:npart, col:col + 1], in_=stat[0:npart, col:col + 1],
                                              func=AF.Exp, scale=-0.5), ["stat"], ["stat"])

        if do_sample:
          with contextlib.ExitStack() as st2:
            def sb2(name, shape, dt=F32):
                return st2.enter_context(nc.sbuf_tensor(name, list(shape), dt))

            GP = min(8, NP)
            NG = NP // GP
            W = GP * 32
            xs_t = sb2("xs_t", [16, D])
            os_s = sb2("os_s", [16, D])
            hsT = sb2("hsT_s", [128, 8, 16], BF16)
            u_ext_s = sb2("u_ext_s", [128, 4, NSEQ, 34])
            pu_ext_s = sb2("pu_ext_s", [128, 4, NSEQ, 19])
            hist_c = sb2("hist_c", [30, NSEQ, 512])
            hist_p = sb2("hist_p", [15, NSEQ, 512])
            gate_s = sb2("gate_s", [128, 4, 16], BF16)
            ycs = sb2("ycs", [128, 4, 16])
            lnb_s = sb2("lnb_s", [128, 2, 16], BF16)
            stats_s = sb2("stats_s", [128, 2, 16])
            sconv_s = sb2("sconv_s", [128, 4, 16], BF16)
            ptmp_s = [sb2("ptmp_s%d" % i, [128, NSEQ, 19]) for i in range(2)]
            dT_s = sb2("dT_s", [128, 4, 16], BF16)
            mixTs = sb2("mixTs", [128, 16, 16], BF16)
            qTs = sb2("qTs", [128, 4, 16], BF16)
            kTs = sb2("kTs", [128, 4, 16], BF16)
            mqTs = sb2("mqTs", [128, 4, 16], BF16)
            v_tok = sb2("v_tok", [16, 512], BF16)
            Qblk = sb2("Qblk", [128, 4, NSEQ, 8], BF16)
            bias32 = sb2("bias32", [128, 32])
            bias_rep = sb2("bias_rep", [128, W])
            maskf = sb2("maskf", [16, NSEQ * 32])
            masknew = sb2("masknew", [16, NSEQ, 32], BF16)
            ptab_i = sb2("ptab_i", [128, NSEQ * NP], I32)
            ptab_f = sb2("ptab_f", [128, NSEQ * NP])
            rows_i = sb2("rows_i", [128, NSEQ * NP], I32)
            iot_i = sb2("iot_i", [128, 1], I32)
            iot_f = sb2("iot_f", [128, 2])
            kvpg = [sb2("kvpg%d" % i, [128, 1024], BF16) for i in range(3 * GP)]
            kTpg = [sb2("kTpg%d" % i, [128, 8, 128], BF16) for i in range(2)]
            zsb = sb2("zsb", [128, W])
            es = sb2("es", [128, W])
            sps = sb2("sps", [128, W], BF16)
            scan = [sb2("scan%d" % i, [128, 2 * W]) for i in range(2)]
            args_t = sb2("args_t", [128, W])
            aTs = sb2("aTs", [128, W], BF16)
            C_in = sb2("C_in", [128, 32])
            ysT = sb2("ysT", [128, 4, 16])
            mkb = sb2("mkb", [128, 2, 512], BF16)
            mvbs = sb2("mvbs", [128, 2, 512], BF16)
            mkTs = sb2("mkTs", [128, 4, NMEM], BF16)
            pTs = sb2("pTs", [128, 32], BF16)
            rden_s = sb2("rden_s", [128, 16])
            g2_s = sb2("g2_s", [128, 16])

            S.add(SP, lambda e: e.dma_start(out=xs_t[:], in_=xs), writes=["xs_t"], dma=True, chan="ld")
            S.add(SP, lambda e: e.dma_start(out=ptab_i[:], in_=ptab.partition_broadcast(128)), writes=["ptab_i"], dma=True, chan="ld")
            S.add(DVE, lambda e: e.tensor_copy(out=ptab_f[:], in_=ptab_i[:]), ["ptab_i"], ["ptab_f"])
            pool_op(lambda e: e.iota(iot_i[:], pattern=[[0, 1]], base=0, channel_multiplier=1), writes=["iot_i"])
            S.add(DVE, lambda e: e.tensor_copy(out=iot_f[:, 0:1], in_=iot_i[:]), ["iot_i"], ["iot_f"])
            S.add(DVE, lambda e: e.tensor_scalar(out=iot_f[:, 1:2], in0=iot_f[:, 0:1], scalar1=float(NPHYS * 128), scalar2=None,
                                                 op0=ALU.add), ["iot_f"], ["iot_f"])
            pool_op(lambda e: e.memset(maskf[:], 1.0), writes=["maskf"])
            pool_op(lambda e: e.affine_select(out=maskf[:].rearrange("p (b h t) -> p b h t", b=NSEQ, h=8), in_=maskf[:].rearrange("p (b h t) -> p b h t", b=NSEQ, h=8),
                                              pattern=[[4, NSEQ], [0, 8], [1, 4]], compare_op=ALU.is_ge, fill=0.0, base=-1,
                                              channel_multiplier=-1), ["maskf"], ["maskf"])
            pool_op(lambda e: e.affine_select(out=maskf[:].rearrange("p (b h t) -> p b h t", b=NSEQ, h=8), in_=maskf[:].rearrange("p (b h t) -> p b h t", b=NSEQ, h=8),
                                              pattern=[[-4, NSEQ], [0, 8], [0, 4]], compare_op=ALU.is_ge, fill=0.0, base=0,
                                              channel_multiplier=1), ["maskf"], ["maskf"])
            S.add(DVE, lambda e: e.tensor_copy(out=masknew[:].rearrange("p b c -> p (b c)"), in_=maskf[:]), ["maskf"], ["masknew"])
            for i in range(2):
                pool_op(lambda e, i=i: e.memset(scan[i][:], 0.0), writes=["scan%d" % i])

            def v4(ap16):
                return ap16.rearrange("p (b t) -> p b t", b=NSEQ)

            def sproj(l, col0, evac):
                i = load_win(l, col0)
                wn = "win%d" % i
                for ch in range(4):
                    b = ch % 2
                    for kc in range(8):
                        S.add(PE, lambda e, i=i, ch=ch, kc=kc, b=b: e.matmul(
                            bank[b][:, 0:16], lhsT=win[i][:, kc, ch * 128:(ch + 1) * 128], rhs=hsT[:, kc, :],
                            start=(kc == 0), stop=(kc == 7), skip_group_check=True), ["hsT", wn], [BN[b]])
                    evac(ch, b)
                return i

            def sproj_tok(l, i, dst, keep=None):
                wn = "win%d" % i
                for kc in range(8):
                    S.add(PE, lambda e, kc=kc: e.matmul(bank[0][0:16, :], lhsT=hsT[:, kc, :], rhs=win[i][:, kc, :],
                                                        start=(kc == 0), stop=(kc == 7), skip_group_check=True), ["hsT", wn], [BN[0]])
                S.add(ACT, lambda e: e.activation(out=kvst[0][0:16, :], in_=bank[0][0:16, :], func=AF.Copy), [BN[0]], ["kvst0"])
                S.add(SP, lambda e: e.dma_start(out=dst[l], in_=kvst[0][0:16, :]), ["kvst0"], dma=True, chan="st")
                if keep is not None:
                    S.add(DVE, lambda e: e.tensor_copy(out=keep[:], in_=kvst[0][0:16, :]), ["kvst0"], ["v_tok"])

            def sample_layer(l):
                load_layer_consts(l)
                S.add(ACT, lambda e: e.activation(out=sqj[0:16, :], in_=xs_t[:], func=AF.Square, accum_out=stat[0:16, 0:1]),
                      ["xs_t"], ["sqj", "stat"])
                S.add(ACT, lambda e: e.activation(out=stat[0:16, 0:1], in_=stat[0:16, 0:1], func=AF.Ln, scale=1.0 / D,
                                                  bias=epst[0:16, :]), ["stat", "epst"], ["stat"])
                S.add(ACT, lambda e: e.activation(out=stat[0:16, 0:1], in_=stat[0:16, 0:1], func=AF.Exp, scale=-0.5), ["stat"], ["stat"])
                S.add(DVE, lambda e: e.scalar_tensor_tensor(out=hs[0][0:16, :], in0=xs_t[:], scalar=stat[0:16, 0:1], in1=gpre[0:16, :],
                                                            op0=ALU.mult, op1=ALU.mult), ["xs_t", "stat", "gpre"], ["hs0"])
                for kc in range(8):
                    S.add(PE, lambda e, kc=kc: e.transpose(out=bankT[:, kc * 16:(kc + 1) * 16], in_=hs[0][0:16, kc * 128:(kc + 1) * 128],
                                                           identity=identb[0:16, 0:16]), ["hs0", "identb"], [BT])
                S.add(ACT, lambda e: e.activation(out=hsT[:], in_=bankT[:, 0:128].rearrange("p (k t) -> p k t", k=8), func=AF.Copy),
                      [BT], ["hsT"])

                S.add(SP, lambda e: e.dma_start(out=hist_c[:], in_=stc[l].rearrange("b k c -> k b c")), writes=["hist_c"], dma=True, chan="ld")
                for b in range(NSEQ):
                    for ch in range(4):
                        S.add(PE, lambda e, b=b, ch=ch: e.transpose(out=bank[0][:, (b * 4 + ch) * 32:(b * 4 + ch) * 32 + 30],
                                                                   in_=hist_c[0:30, b, ch * 128:(ch + 1) * 128], identity=identf[0:30, 0:30]),
                              ["hist_c", "identf"], [BN[0]])
                for b in range(NSEQ):
                    S.add(ACT, lambda e, b=b: e.activation(out=u_ext_s[:, :, b, 0:30],
                                                          in_=bank[0][:, b * 128:(b + 1) * 128].rearrange("p (c k) -> p c k", c=4)[:, :, 0:30],
                                                          func=AF.Copy), [BN[0]], ["u_ext_s"])
                sproj(l, 512, lambda ch, b: S.add(
                    ACT, lambda e: e.activation(out=u_ext_s[:, ch, :, 30:34], in_=v4(bank[b][:, 0:16]), func=AF.Sigmoid),
                    [BN[b]], ["u_ext_s"]))
                sproj(l, 0, lambda ch, b: S.add(
                    DVE, lambda e: e.tensor_tensor(out=u_ext_s[:, ch, :, 30:34], in0=v4(bank[b][:, 0:16]), in1=u_ext_s[:, ch, :, 30:34],
                                                   op=ALU.mult), [BN[b], "u_ext_s"], ["u_ext_s"]))
                sproj(l, 1024, lambda ch, b: S.add(
                    ACT, lambda e: e.activation(out=gate_s[:, ch, :], in_=bank[b][:, 0:16], func=AF.Silu), [BN[b]], ["gate_s"]))
                for ch in range(4):
                    S.add(DVE, lambda e, ch=ch: e.tensor_scalar(out=v4(ycs[:, ch, :]), in0=u_ext_s[:, ch, :, 0:4], scalar1=ctab[:, ch, 0:1],
                                                               scalar2=ctab[:, ch, 31:32], op0=ALU.mult, op1=ALU.add),
                          ["u_ext_s", "ctab"], ["ycs"])
                    for k in range(1, 31):
                        S.add(DVE, lambda e, ch=ch, k=k: e.scalar_tensor_tensor(
                            out=v4(ycs[:, ch, :]), in0=u_ext_s[:, ch, :, k:k + 4], scalar=ctab[:, ch, k:k + 1], in1=v4(ycs[:, ch, :]),
                            op0=ALU.mult, op1=ALU.add), ["u_ext_s", "ctab", "ycs"], ["ycs"])
                for ch in range(4):
                    S.add(ACT, lambda e, ch=ch: e.activation(out=lnb_s[:, 0, :], in_=ycs[:, ch, :], func=AF.Copy), ["ycs"], ["lnb_s0"])
                    S.add(ACT, lambda e, ch=ch: e.activation(out=lnb_s[:, 1, :], in_=ycs[:, ch, :], func=AF.Square), ["ycs"], ["lnb_s1"])
                    S.add(PE, lambda e, ch=ch: e.matmul(bank[0][:, 0:16], lhsT=inv512[:], rhs=lnb_s[:, 0, :], start=(ch == 0), stop=(ch == 3),
                                                        skip_group_check=True), ["lnb_s0", "inv512"], [BN[0]])
                    S.add(PE, lambda e, ch=ch: e.matmul(bank[1][:, 0:16], lhsT=inv512[:], rhs=lnb_s[:, 1, :], start=(ch == 0), stop=(ch == 3),
                                                        skip_group_check=True), ["lnb_s1", "inv512"], [BN[1]])
                S.add(ACT, lambda e: e.activation(out=stats_s[:, 0, :], in_=bank[0][:, 0:16], func=AF.Copy), [BN[0]], ["stats_s"])
                S.add(ACT, lambda e: e.activation(out=stats_s[:, 1, :], in_=bank[0][:, 0:16], func=AF.Square), [BN[0]], ["stats_s"])
                S.add(DVE, lambda e: e.tensor_tensor(out=stats_s[:, 1, :], in0=bank[1][:, 0:16], in1=stats_s[:, 1, :], op=ALU.subtract),
                      [BN[1], "stats_s"], ["stats_s"])
                S.add(ACT, lambda e: e.activation(out=stats_s[:, 1, :], in_=stats_s[:, 1, :], func=AF.Ln, bias=epst[:]), ["stats_s", "epst"], ["stats_s"])
                S.add(ACT, lambda e: e.activation(out=stats_s[:, 1, :], in_=stats_s[:, 1, :], func=AF.Exp, scale=-0.5), ["stats_s"], ["stats_s"])
                for ch in range(4):
                    S.add(DVE, lambda e, ch=ch: e.tensor_tensor(out=ycs[:, ch, :], in0=ycs[:, ch, :], in1=stats_s[:, 0, :], op=ALU.subtract),
                          ["ycs", "stats_s"], ["ycs"])
                    S.add(DVE, lambda e, ch=ch: e.tensor_tensor(out=ycs[:, ch, :], in0=ycs[:, ch, :], in1=stats_s[:, 1, :], op=ALU.mult),
                          ["ycs", "stats_s"], ["ycs"])
                    S.add(ACT, lambda e, ch=ch: e.activation(out=sconv_s[:, ch, :], in_=ycs[:, ch, :], func=AF.Silu,
                                                            scale=ctab[:, ch, 32:33], bias=ctab[:, ch, 33:34]), ["ycs", "ctab"], ["sconv_s"])
                for co in range(4):
                    b = co % 2
                    for ch in range(4):
                        S.add(PE, lambda e, co=co, ch=ch, b=b: e.matmul(bank[b][:, 0:16], lhsT=wpw[:, ch, co * 128:(co + 1) * 128],
                                                                        rhs=sconv_s[:, ch, :], start=(ch == 0), stop=(ch == 3),
                                                                        skip_group_check=True), ["sconv_s", "wpw"], [BN[b]])
                    S.add(DVE, lambda e, co=co, b=b: e.scalar_tensor_tensor(out=mixTs[:, co, :], in0=bank[b][:, 0:16], scalar=ctab[:, co, 34:35],
                                                                            in1=gate_s[:, co, :], op0=ALU.add, op1=ALU.mult),
                          [BN[b], "ctab", "gate_s"], ["mixTs"])
                for b in range(NSEQ):
                    for ch in range(4):
                        S.add(PE, lambda e, b=b, ch=ch: e.transpose(out=bank[1][0:30, ch * 128:(ch + 1) * 128], in_=u_ext_s[:, ch, b, 4:34],
                                                                   identity=identf[:]), ["u_ext_s", "identf"], [BN[1]])
                    S.add(ACT, lambda e: e.activation(out=kvst[1][0:30, :], in_=bank[1][0:30, :], func=AF.Copy), [BN[1]], ["kvst1"])
                    S.add(SP, lambda e, b=b: e.dma_start(out=conv_s[l, b], in_=kvst[1][0:30, :]), ["kvst1"], dma=True, chan="st")

                S.add(SP, lambda e: e.dma_start(out=hist_p[:], in_=stp[l].rearrange("b k c -> k b c")), writes=["hist_p"], dma=True, chan="ld")
                for b in range(NSEQ):
                    for ch in range(4):
                        S.add(PE, lambda e, b=b, ch=ch: e.transpose(out=bank[0][:, (b * 4 + ch) * 16:(b * 4 + ch) * 16 + 15],
                                                                   in_=hist_p[0:15, b, ch * 128:(ch + 1) * 128], identity=identf[0:15, 0:15]),
                              ["hist_p", "identf"], [BN[0]])
                for b in range(NSEQ):
                    S.add(ACT, lambda e, b=b: e.activation(out=pu_ext_s[:, :, b, 0:15],
                                                          in_=bank[0][:, b * 64:(b + 1) * 64].rearrange("p (c k) -> p c k", c=4)[:, :, 0:15],
                                                          func=AF.Copy), [BN[0]], ["pu_ext_s"])
                sproj(l, 1536, lambda ch, b: S.add(
                    ACT, lambda e: e.activation(out=pu_ext_s[:, ch, :, 15:19], in_=v4(bank[b][:, 0:16]), func=AF.Copy), [BN[b]], ["pu_ext_s"]))
                sproj(l, 2048, lambda ch, b: S.add(
                    ACT, lambda e: e.activation(out=gate_s[:, ch, :], in_=bank[b][:, 0:16], func=AF.Silu), [BN[b]], ["gate_s"]))
                for g, w in enumerate(POOLW):
                    cur = pu_ext_s[:, g, :, :]
                    curn = "pu_ext_s"
                    lo, d, k = 0, 1, 0
                    while d < w:
                        nxt = ptmp_s[k % 2][:]
                        nn = "ptmp_s%d" % (k % 2)
                        pool_op(lambda e, cur=cur, nxt=nxt, lo=lo, d=d: e.tensor_tensor(
                            out=nxt[:, :, lo + d:19], in0=cur[:, :, lo + d:19], in1=cur[:, :, lo:19 - d], op=ALU.add), [curn], [nn])
                        cur, curn = nxt, nn
                        lo += d
                        d *= 2
                        k += 1
                    S.add(DVE, lambda e, cur=cur, g=g, w=w: e.scalar_tensor_tensor(
                        out=v4(dT_s[:, g, :]), in0=cur[:, :, 15:19], scalar=1.0 / w, in1=pu_ext_s[:, g, :, 15:19],
                        op0=ALU.mult, op1=ALU.subtract), [curn, "pu_ext_s"], ["dT_s"])
                    b = g % 2
                    S.add(PE, lambda e, g=g, b=b: e.matmul(bank[b][:, 0:16], lhsT=wpool[:, g, :], rhs=dT_s[:, g, :], start=True, stop=True,
                                                           skip_group_check=True), ["dT_s", "wpool"], [BN[b]])
                    S.add(DVE, lambda e, g=g, b=b: e.scalar_tensor_tensor(out=mixTs[:, 4 + g, :], in0=bank[b][:, 0:16], scalar=ctab[:, g, 35:36],
                                                                          in1=gate_s[:, g, :], op0=ALU.mult, op1=ALU.mult),
                          [BN[b], "ctab", "gate_s"], ["mixTs"])
                for b in range(NSEQ):
                    for ch in range(4):
                        S.add(PE, lambda e, b=b, ch=ch: e.transpose(out=bank[1][0:15, ch * 128:(ch + 1) * 128], in_=pu_ext_s[:, ch, b, 4:19],
                                                                   identity=identf[:]), ["pu_ext_s", "identf"], [BN[1]])
                    S.add(ACT, lambda e: e.activation(out=kvst[1][0:15, :], in_=bank[1][0:15, :], func=AF.Copy), [BN[1]], ["kvst1"])
                    S.add(SP, lambda e, b=b: e.dma_start(out=pool_s[l, b], in_=kvst[1][0:15, :]), ["kvst1"], dma=True, chan="st")

                sproj(l, 4608, lambda ch, b: S.add(
                    ACT, lambda e: e.activation(out=mqTs[:, ch, :], in_=bank[b][:, 0:16], func=AF.Copy, scale=128 ** -0.5), [BN[b]], ["mqTs"]))
                sproj(l, 5120, lambda ch, b: S.add(
                    ACT, lambda e: e.activation(out=gate_s[:, ch, :], in_=bank[b][:, 0:16], func=AF.Silu), [BN[b]], ["gate_s"]))
                for b in range(NSEQ):
                    S.add(POOL, lambda e, b=b: e.dma_start(out=mkb[:], in_=cmk[l, b].rearrange("(c m) f -> m c f", c=2)),
                          writes=["mkb"], dma=True, chan="w")
                    S.add(POOL, lambda e, b=b: e.dma_start(out=mvbs[:], in_=cmv[l, b].rearrange("(c m) f -> m c f", c=2)),
                          writes=["mvbs"], dma=True, chan="w")
                    for hh in range(4):
                        for mc in range(2):
                            S.add(PE, lambda e, hh=hh, mc=mc: e.transpose(out=bankT[:, (hh * 2 + mc) * 128:(hh * 2 + mc + 1) * 128],
                                                                         in_=mkb[:, mc, hh * 128:(hh + 1) * 128], identity=identb[:]),
                                  ["mkb", "identb"], [BT])
                    S.add(ACT, lambda e: e.activation(out=mkTs[:].rearrange("p h m -> p (h m)"), in_=bankT[:], func=AF.Copy), [BT], ["mkTs"])
                    for hh in range(4):
                        for mc in range(2):
                            c = (hh * 2 + mc) * 4
                            S.add(PE, lambda e, hh=hh, mc=mc, c=c, b=b: e.matmul(
                                bank[2][:, c:c + 4], lhsT=mkTs[:, hh, mc * 128:(mc + 1) * 128], rhs=mqTs[:, hh, b * 4:(b + 1) * 4],
                                start=True, stop=True, skip_group_check=True), ["mkTs", "mqTs"], [BN[2]])
                    S.add(ACT, lambda e: e.activation(out=pTs[:], in_=bank[2][:, 0:32], func=AF.Exp), [BN[2]], ["pTs"])
                    for hh in range(4):
                        for mc in range(2):
                            c = (hh * 2 + mc) * 4
                            S.add(PE, lambda e, hh=hh, mc=mc, c=c: e.matmul(
                                bank[4][:, hh * 4:(hh + 1) * 4], lhsT=mvbs[:, mc, hh * 128:(hh + 1) * 128], rhs=pTs[:, c:c + 4],
                                start=(mc == 0), stop=(mc == 1), skip_group_check=True), ["mvbs", "pTs"], [BN[4]])
                            S.add(PE, lambda e, hh=hh, mc=mc, c=c: e.matmul(
                                bank[5][:, hh * 4:(hh + 1) * 4], lhsT=onesb[:], rhs=pTs[:, c:c + 4],
                                start=(mc == 0), stop=(mc == 1), skip_group_check=True), ["onesb", "pTs"], [BN[5]])
                    S.add(DVE, lambda e: e.reciprocal(out=rden_s[:], in_=bank[5][:, 0:16]), [BN[5]], ["rden_s"])
                    pool_op(lambda e, b=b: e.tensor_tensor(out=g2_s[:].rearrange("p (h t) -> p h t", h=4),
                                                           in0=rden_s[:].rearrange("p (h t) -> p h t", h=4),
                                                           in1=gate_s[:, :, b * 4:(b + 1) * 4], op=ALU.mult), ["rden_s", "gate_s"], ["g2_s"])
                    S.add(DVE, lambda e, b=b: e.tensor_tensor(out=mixTs[:, 12:16, b * 4:(b + 1) * 4],
                                                             in0=bank[4][:, 0:16].rearrange("p (h t) -> p h t", h=4),
                                                             in1=g2_s[:].rearrange("p (h t) -> p h t", h=4), op=ALU.mult),
                          [BN[4], "g2_s"], ["mixTs"])

                sproj(l, 2560, lambda ch, b: S.add(
                    ACT, lambda e: e.activation(out=qTs[:, ch, :], in_=bank[b][:, 0:16], func=AF.Copy, scale=0.125), [BN[b]], ["qTs"]))
                iw = sproj(l, 3072, lambda ch, b: S.add(
                    ACT, lambda e: e.activation(out=kTs[:, ch, :], in_=bank[b][:, 0:16], func=AF.Copy), [BN[b]], ["kTs"]))
                sproj_tok(l, iw, k_s)
                iw = load_win(l, 3584)
                sproj_tok(l, iw, v_s, keep=v_tok)
                sproj(l, 4096, lambda ch, b: S.add(
                    ACT, lambda e: e.activation(out=gate_s[:, ch, :], in_=bank[b][:, 0:16], func=AF.Silu), [BN[b]], ["gate_s"]))
                pool_op(lambda e: e.memset(Qblk[:], 0.0), ["Qblk"], ["Qblk"])
                S.add(DVE, lambda e: e.tensor_copy(out=Qblk[0:64, :, :, 0:4], in_=qTs[0:64, :, :].rearrange("p c (b t) -> p c b t", b=NSEQ)),
                      ["qTs", "Qblk"], ["Qblk"])
                S.add(DVE, lambda e: e.tensor_copy(out=Qblk[64:128, :, :, 4:8], in_=qTs[64:128, :, :].rearrange("p c (b t) -> p c b t", b=NSEQ)),
                      ["qTs", "Qblk"], ["Qblk"])
                for t in range(4):
                    S.add(DVE, lambda e, t=t: e.tensor_copy(out=bias32[:].rearrange("p (h t) -> p h t", t=4)[:, :, t], in_=sbb_t[:, :]),
                          ["sbb_t", "bias32"], ["bias32"])
                for jj in range(GP):
                    S.add(DVE, lambda e, jj=jj: e.tensor_copy(out=bias_rep[:, jj * 32:(jj + 1) * 32], in_=bias32[:]), ["bias32", "bias_rep"], ["bias_rep"])
                S.add(DVE, lambda e: e.tensor_scalar(out=rows_f[:], in0=ptab_f[:], scalar1=128.0, scalar2=iot_f[:, l:l + 1],
                                                     op0=ALU.mult, op1=ALU.add), ["ptab_f", "iot_f"], ["rows_f"])
                S.add(DVE, lambda e: e.tensor_copy(out=rows_i[:], in_=rows_f[:]), ["rows_f"], ["rows_i"])

                def new_block(b):
                    for hp in range(4):
                        S.add(PE, lambda e, hp=hp: e.matmul(bank[4][0:16, hp * 8:(hp + 1) * 8], lhsT=kTs[:, hp, :], rhs=Qblk[:, hp, b, :],
                                                            start=True, stop=True, skip_group_check=True), ["kTs", "Qblk"], [BN[4]])
                    S.add(DVE, lambda e: e.tensor_tensor(out=zsb[0:16, 0:32], in0=bank[4][0:16, 0:32], in1=bias32[0:16, :], op=ALU.add),
                          [BN[4], "bias32"], ["zsb"])
                    S.add(ACT, lambda e: e.activation(out=es[0:16, 0:32], in_=zsb[0:16, 0:32], func=AF.Exp), ["zsb"], ["es"])
                    S.add(ACT, lambda e: e.activation(out=sps[0:16, 0:32], in_=es[0:16, 0:32], func=AF.Ln, bias=1.0), ["es"], ["sps"])
                    pool_op(lambda e: e.tensor_tensor(out=sps[0:16, 0:32], in0=sps[0:16, 0:32], in1=masknew[:, b, :], op=ALU.mult),
                            ["sps", "masknew"], ["sps"])
                    S.add(PE, lambda e: e.matmul(bank[4][0:16, 0:32], lhsT=negU[0:16, 0:16], rhs=sps[0:16, 0:32], start=True, stop=True,
                                                 skip_group_check=True), ["sps", "negU", "zsb"], [BN[4]])
                    S.add(PE, lambda e: e.matmul(bank[5][:, 0:32], lhsT=onesb[0:16, :], rhs=sps[0:16, 0:32], start=True, stop=True,
                                                 skip_group_check=True), ["sps", "onesb"], [BN[5]])
                    S.add(DVE, lambda e: e.tensor_tensor(out=args_t[0:16, 0:32], in0=bank[4][0:16, 0:32], in1=zsb[0:16, 0:32], op=ALU.add),
                          [BN[4], "zsb"], ["args_t"])
                    S.add(ACT, lambda e: e.activation(out=aTs[0:16, 0:32], in_=args_t[0:16, 0:32], func=AF.Exp), ["args_t"], ["aTs"])
                    pool_op(lambda e: e.tensor_tensor(out=aTs[0:16, 0:32], in0=aTs[0:16, 0:32], in1=masknew[:, b, :], op=ALU.mult),
                            ["aTs", "masknew"], ["aTs"])
                    for hp in range(4):
                        S.add(PE, lambda e, hp=hp: e.matmul(bank[6][:, hp * 8:(hp + 1) * 8], lhsT=v_tok[0:16, hp * 128:(hp + 1) * 128],
                                                            rhs=aTs[0:16, hp * 8:(hp + 1) * 8], start=(hp == 0), stop=False,
                                                            skip_group_check=True), ["v_tok", "aTs"], [BN[6]])
                    S.add(DVE, lambda e: e.tensor_copy(out=C_in[:], in_=bank[5][:, 0:32]), [BN[5]], ["C_in"])

                def scores(u):
                    b, G, gi = u
                    zb = 2 + gi % 2
                    for jp in range(0, GP, 2):
                        pages = [jp, jp + 1] if jp + 1 < GP else [jp]
                        kps = []
                        for jj in pages:
                            col = b * NP + G * GP + jj
                            vi = (gi % 3) * GP + jj
                            kp, kn = kvpg[vi], "kvpg%d" % vi
                            S.add(POOL, lambda e, kp=kp, col=col: e.indirect_dma_start(
                                out=kp[:], out_offset=None, in_=ckv, in_offset=bass.IndirectOffsetOnAxis(ap=rows_i[:, col:col + 1], axis=0)),
                                ["rows_i"], [kn], dma=True, chan="kv%d" % l)
                            kps.append((kp, kn))
                        for pi, (kp, kn) in enumerate(kps):
                            for hp in range(4):
                                o = (pi * 4 + hp) * 128
                                S.add(PE, lambda e, kp=kp, hp=hp, o=o: e.transpose(out=bankT[:, o:o + 128], in_=kp[:, hp * 128:(hp + 1) * 128],
                                                                                identity=identb[:]), [kn, "identb"], [BT])
                        kt, ktn = kTpg[ring[1] % 2], "kTpg%d" % (ring[1] % 2)
                        ev = ACT if ring[1] % 2 == 0 else DVE
                        ring[1] += 1
                        ncol = 512 * len(pages)
                        if ev == ACT:
                            S.add(ACT, lambda e, kt=kt, ncol=ncol: e.activation(out=kt[:].rearrange("p c s -> p (c s)")[:, 0:ncol],
                                                                            in_=bankT[:, 0:ncol], func=AF.Copy), [BT], [ktn])
                        else:
                            S.add(DVE, lambda e, kt=kt, ncol=ncol: e.tensor_copy(out=kt[:].rearrange("p c s -> p (c s)")[:, 0:ncol],
                                                                             in_=bankT[:, 0:ncol]), [BT], [ktn])
                        for pi, jj in enumerate(pages):
                            for hp in range(4):
                                c = jj * 32 + hp * 8
                                S.add(PE, lambda e, kt=kt, hp=hp, c=c, pi=pi: e.matmul(bank[zb][:, c:c + 8], lhsT=kt[:, pi * 4 + hp, :],
                                                                                    rhs=Qblk[:, hp, b, :], start=True, stop=True,
                                                                                    skip_group_check=True), [ktn, "Qblk"], [BN[zb]])

                def chain_av(u):
                    b, G, gi = u
                    zb = 2 + gi % 2
                    S.add(DVE, lambda e: e.tensor_tensor(out=zsb[:], in0=bank[zb][:, 0:W], in1=bias_rep[:], op=ALU.add),
                          [BN[zb], "bias_rep"], ["zsb"])
                    S.add(ACT, lambda e: e.activation(out=es[:], in_=zsb[:], func=AF.Exp), ["zsb"], ["es"])
                    S.add(ACT, lambda e: e.activation(out=sps[:], in_=es[:], func=AF.Ln, bias=1.0), ["es"], ["sps"])
                    S.add(PE, lambda e: e.matmul(bank[4][:, 0:W], lhsT=negU[:], rhs=sps[:], start=True, stop=True, skip_group_check=True),
                          ["sps", "negU"], [BN[4]])
                    S.add(PE, lambda e: e.matmul(bank[5][:, 0:W], lhsT=onesb[:], rhs=sps[:], start=True, stop=True, skip_group_check=True),
                          ["sps", "onesb"], [BN[5]])
                    if GP > 1:
                        S.add(DVE, lambda e: e.tensor_copy(out=scan[0][:, 0:W - 32], in_=bank[5][:, 32:W]), [BN[5], "scan0"], ["scan0"])
                    S.add(DVE, lambda e: e.tensor_copy(out=scan[0][:, W - 32:W], in_=C_in[:]), ["C_in", "scan0"], ["scan0"])
                    src, d = 0, 1
                    while d < GP:
                        S.add(DVE, lambda e, src=src, d=d: e.tensor_tensor(out=scan[1 - src][:, 0:W], in0=scan[src][:, 0:W],
                                                                          in1=scan[src][:, d * 32:W + d * 32], op=ALU.add),
                              ["scan%d" % src, "scan%d" % (1 - src)], ["scan%d" % (1 - src)])
                        src = 1 - src
                        d *= 2
                    sf, sfn = scan[src], "scan%d" % src
                    S.add(DVE, lambda e, sf=sf: e.tensor_tensor(out=C_in[:], in0=bank[5][:, 0:32], in1=sf[:, 0:32], op=ALU.add),
                          [BN[5], sfn, "C_in"], ["C_in"])
                    S.add(DVE, lambda e, sf=sf: e.tensor_tensor(out=args_t[:], in0=bank[4][:, 0:W], in1=sf[:, 0:W], op=ALU.subtract),
                          [BN[4], sfn], ["args_t"])
                    pool_op(lambda e: e.tensor_tensor(out=args_t[:], in0=args_t[:], in1=zsb[:], op=ALU.add), ["args_t", "zsb"], ["args_t"])
                    S.add(ACT, lambda e: e.activation(out=aTs[:], in_=args_t[:], func=AF.Exp), ["args_t"], ["aTs"])
                    for jj in range(GP):
                        vi = (gi % 3) * GP + jj
                        for hp in range(4):
                            c = jj * 32 + hp * 8
                            S.add(PE, lambda e, vi=vi, hp=hp, c=c: e.matmul(bank[6][:, hp * 8:(hp + 1) * 8], lhsT=kvpg[vi][:, 512 + hp * 128:512 + (hp + 1) * 128],
                                                                          rhs=aTs[:, c:c + 8], start=False, stop=False, skip_group_check=True),
                                  ["kvpg%d" % vi, "aTs"], [BN[6]])

                def extract(b):
                    S.add(ACT, lambda e: e.activation(out=ysT[0:64, :, b * 4:(b + 1) * 4],
                                                      in_=bank[6][0:64, 0:32].rearrange("p (c x) -> p c x", c=4)[:, :, 0:4], func=AF.Copy),
                          [BN[6]], ["ysT"])
                    S.add(ACT, lambda e: e.activation(out=ysT[64:128, :, b * 4:(b + 1) * 4],
                                                      in_=bank[6][64:128, 0:32].rearrange("p (c x) -> p c x", c=4)[:, :, 4:8], func=AF.Copy),
                          [BN[6]], ["ysT"])

                units = []
                for b in range(NSEQ):
                    for G in range(NG - 1, -1, -1):
                        units.append((b, G, len(units)))
                scores(units[0])
                for ui, u in enumerate(units):
                    if ui + 1 < len(units):
                        scores(units[ui + 1])
                    if u[1] == NG - 1:
                        new_block(u[0])
                    chain_av(u)
                    if u[1] == 0:
                        extract(u[0])
                S.add(DVE, lambda e: e.tensor_tensor(out=mixTs[:, 8:12, :], in0=ysT[:], in1=gate_s[:], op=ALU.mult), ["ysT", "gate_s"], ["mixTs"])

                for half in range(2):
                    S.add(POOL, lambda e, half=half: e.dma_start(
                        out=wout[:], in_=w_out[l][:, half * 512:(half + 1) * 512].rearrange("(q p) f -> p q f", p=128)),
                        writes=["wout"], dma=True, chan="w")
                    for f in range(16):
                        S.add(PE, lambda e, f=f, half=half: e.matmul(bank[half][0:16, :], lhsT=mixTs[:, f, :], rhs=wout[:, f, :], start=(f == 0),
                                                                    stop=(f == 15), skip_group_check=True), ["mixTs", "wout"], [BN[half]])
                    S.add(ACT, lambda e, half=half: e.activation(out=os_s[:, half * 512:(half + 1) * 512], in_=bank[half][0:16, :], func=AF.Copy),
                          [BN[half]], ["os_s"])
                S.add(ACT, lambda e: e.activation(out=sqj[0:16, :], in_=os_s[:], func=AF.Square, accum_out=stat[0:16, 2:3]), ["os_s"], ["sqj", "stat"])
                S.add(ACT, lambda e: e.activation(out=stat[0:16, 2:3], in_=stat[0:16, 2:3], func=AF.Ln, scale=1.0 / D, bias=epst[0:16, :]),
                      ["stat", "epst"], ["stat"])
                S.add(ACT, lambda e: e.activation(out=stat[0:16, 2:3], in_=stat[0:16, 2:3], func=AF.Exp, scale=-0.5), ["stat"], ["stat"])
                S.add(DVE, lambda e: e.scalar_tensor_tensor(out=os_s[:], in0=os_s[:], scalar=stat[0:16, 2:3], in1=gpost[0:16, :],
                                                            op0=ALU.mult, op1=ALU.mult), ["os_s", "stat", "gpost"], ["os_s"])
                pool_op(lambda e: e.tensor_tensor(out=xs_t[:], in0=xs_t[:], in1=os_s[:], op=ALU.add), ["xs_t", "os_s"], ["xs_t"])

            ring = [0, 0]
            rows_f = sb2("rows_f", [128, NSEQ * NP])
            for l in range(NL):
                sample_layer(l)
            S.add(SP, lambda e: e.dma_start(out=gmisc[:], in_=final_g[0:1, :].partition_broadcast(128)), writes=["gmisc"], dma=True, chan="ld")
            S.add(ACT, lambda e: e.activation(out=sqj[0:16, :], in_=xs_t[:], func=AF.Square, accum_out=stat[0:16, 0:1]), ["xs_t"], ["sqj", "stat"])
            S.add(ACT, lambda e: e.activation(out=stat[0:16, 0:1], in_=stat[0:16, 0:1], func=AF.Ln, scale=1.0 / D, bias=epst[0:16, :]),
                  ["stat", "epst"], ["stat"])
            S.add(ACT, lambda e: e.activation(out=stat[0:16, 0:1], in_=stat[0:16, 0:1], func=AF.Exp, scale=-0.5), ["stat"], ["stat"])
            S.add(DVE, lambda e: e.scalar_tensor_tensor(out=os_s[:], in0=xs_t[:], scalar=stat[0:16, 0:1], in1=gmisc[0:16, :],
                                                        op0=ALU.mult, op1=ALU.mult), ["xs_t", "stat", "gmisc", "os_s"], ["os_s"])
            S.add(SP, lambda e: e.dma_start(out=y_s, in_=os_s[:]), ["os_s"], dma=True, chan="st")
          S.barrier()

        if do_prompt:
            xt = sb("xt", [128, 4, D])
            x1 = nc.dram_tensor("x1", [SEQ, D], F32).ap()
            kT_all = sb("kT_all", [128, 4, SEQ], BF16)
            v_all = sb("v_all", [128, 16, 512], BF16)
            hT = sb("hT", [128, 8, 512], BF16)
            mixT = sb("mixT", [128, 16, 512], BF16)
            gate = sb("gate", [128, 4, 512], BF16)
            u_ext = sb("u_ext", [128, 4, 542])
            yflat = sb("yconv", [128, 2048])
            yconv = yflat[:].rearrange("p (c t) -> p c t", c=4)
            sconv = sb("sconv", [128, 4, 512], BF16)
            stats = sb("stats", [128, 2, 512])
            pu_ext = sb("pu_ext", [128, 4, 527])
            ptmp = [yflat[:, 0:527], yflat[:, 1024:1551]]
            dT = sconv
            corr = sb("corr", [128, 4, 16])
            qT = sb("qT", [128, 4, 512], BF16)
            esb = [sb("esb%d" % i, [128, 512]) for i in range(2)]
            spb = [sb("spb%d" % i, [128, 512], BF16) for i in range(2)]
            aTb = [sb("aTb%d" % i, [128, 512], BF16) for i in range(2)]
            spacc = sb("spacc", [128, 512])
            spaccb = sb("spaccb", [128, 512], BF16)
            mkT = sb("mkT", [128, 4, NMEM], BF16)
            mvb = sb("mvb", [128, 2, 512], BF16)
            memhT = hT[:, :, 0:NMEM]
            rden = esb[0]
            gtmp = esb[1]
            outst = kvst[0]

            for g, w in enumerate(POOLW):
                pool_op(lambda e, g=g, w=w: e.memset(corr[:, g, :], 1.0 / w), writes=["corr"])
                for t in range(w - 1):
                    pool_op(lambda e, g=g, t=t: e.memset(corr[:, g, t:t + 1], 1.0 / (t + 1)), ["corr"], ["corr"])


            def mem_kv_layer(l):
                S.add(SP, lambda e: e.dma_start(out=gmisc[:], in_=mem_g[l:l + 1, :].partition_broadcast(128)),
                      writes=["gmisc"], dma=True, chan="ld")
                for mc in range(2):
                    xm = kvst[0]
                    xm2 = kvst[1]
                    S.add(SP, lambda e, mc=mc: e.dma_start(out=xm[:], in_=memp[mc * 128:(mc + 1) * 128, 0:512]),
                          writes=["kvst0"], dma=True, chan="ld")
                    S.add(SP, lambda e, mc=mc: e.dma_start(out=xm2[:], in_=memp[mc * 128:(mc + 1) * 128, 512:1024]),
                          writes=["kvst1"], dma=True, chan="ld")
                    S.add(ACT, lambda e: e.activation(out=sqj[:, 0:512], in_=xm[:], func=AF.Square,
                                                      accum_out=stat[:, 0:1]), ["kvst0"], ["sqj", "stat"])
                    S.add(ACT, lambda e: e.activation(out=sqj[:, 512:1024], in_=xm2[:], func=AF.Square,
                                                      accum_out=stat[:, 1:2]), ["kvst1"], ["sqj", "stat"])
                    S.add(DVE, lambda e: e.tensor_tensor(out=stat[:, 0:1], in0=stat[:, 0:1], in1=stat[:, 1:2], op=ALU.add),
                          ["stat"], ["stat"])
                    S.add(ACT, lambda e: e.activation(out=stat[:, 0:1], in_=stat[:, 0:1], func=AF.Ln,
                                                      scale=1.0 / D, bias=epst[:]), ["stat", "epst"], ["stat"])
                    S.add(ACT, lambda e: e.activation(out=stat[:, 0:1], in_=stat[:, 0:1], func=AF.Exp, scale=-0.5),
                          ["stat"], ["stat"])
                    h = hs[mc % 2]
                    hn = "hs%d" % (mc % 2)
                    S.add(DVE, lambda e, h=h: e.scalar_tensor_tensor(out=h[:, 0:512], in0=xm[:], scalar=stat[:, 0:1],
                                                                      in1=gmisc[:, 0:512], op0=ALU.mult, op1=ALU.mult),
                          ["kvst0", "stat", "gmisc"], [hn])
                    S.add(DVE, lambda e, h=h: e.scalar_tensor_tensor(out=h[:, 512:1024], in0=xm2[:], scalar=stat[:, 0:1],
                                                                      in1=gmisc[:, 512:1024], op0=ALU.mult, op1=ALU.mult),
                          ["kvst1", "stat", "gmisc"], [hn])
                    for kc in range(8):
                        S.add(PE, lambda e, h=h, kc=kc: e.transpose(out=bankT[:, kc * 128:(kc + 1) * 128],
                                                                     in_=h[:, kc * 128:(kc + 1) * 128], identity=identb[:]),
                              [hn, "identb"], [BT])
                    S.add(ACT, lambda e, mc=mc: e.activation(out=memhT[:, :, mc * 128:(mc + 1) * 128],
                                                            in_=bankT[:].rearrange("p (k t) -> p k t", k=8), func=AF.Copy),
                          [BT], ["hT"])
                for half in range(2):
                    i = wctr[0] % 2
                    wctr[0] += 1
                    S.add(POOL, lambda e, half=half, i=i: e.dma_start(
                        out=win[i][:], in_=w_mem[l][:, half * 512:(half + 1) * 512].rearrange("(q p) f -> p q f", p=128)),
                        writes=["win%d" % i], dma=True, chan="w")
                    wn = "win%d" % i
                    for mc in range(2):
                        b = mc % 2
                        for kc in range(8):
                            S.add(PE, lambda e, i=i, mc=mc, kc=kc, b=b: e.matmul(
                                bank[b][:], lhsT=memhT[:, kc, mc * 128:(mc + 1) * 128], rhs=win[i][:, kc, :],
                                start=(kc == 0), stop=(kc == 7), skip_group_check=True),
                                ["hT", wn], [BN[b]])
                        kv = kvst[b]
                        S.add(ACT, lambda e, kv=kv, b=b: e.activation(out=kv[:], in_=bank[b][:], func=AF.Copy),
                              [BN[b]], ["kvst%d" % b])
                        dst = (mk_p if half == 0 else mv_p)
                        S.add(SP, lambda e, kv=kv, dst=dst, mc=mc: e.dma_start(out=dst[l, mc * 128:(mc + 1) * 128, :], in_=kv[:]),
                              ["kvst%d" % b], dma=True, chan="st")
                        if half == 1:
                            S.add(DVE, lambda e, kv=kv, mc=mc: e.tensor_copy(out=mvb[:, mc, :], in_=kv[:]),
                                  ["kvst%d" % b], ["mvb"])
                    if half == 0:
                        for hh in range(4):
                            b = hh % 2
                            for kc in range(8):
                                S.add(PE, lambda e, i=i, hh=hh, kc=kc, b=b: e.matmul(
                                    bank[b][:, 0:NMEM], lhsT=win[i][:, kc, hh * 128:(hh + 1) * 128], rhs=memhT[:, kc, :],
                                    start=(kc == 0), stop=(kc == 7), skip_group_check=True),
                                    ["hT", wn], [BN[b]])
                            S.add(ACT, lambda e, hh=hh, b=b: e.activation(out=mkT[:, hh, :], in_=bank[b][:, 0:NMEM], func=AF.Copy),
                                  [BN[b]], ["mkT"])

            def prompt_tile(l, q):
                t0 = q * 512
                last = (q == 3)
                for i in range(4):
                    ti = 4 * q + i
                    src = xp if l == 0 else x1
                    S.add(SP, lambda e, i=i, ti=ti, src=src: e.dma_start(out=xt[:, i, :], in_=src[ti * 128:(ti + 1) * 128, :]),
                          reads=["x1_%d" % ti], writes=["xt%d" % i], dma=True, chan="ld")
                for i in range(4):
                    S.add(ACT, lambda e, i=i: e.activation(out=sqj[:], in_=xt[:, i, :], func=AF.Square,
                                                            accum_out=stat[:, 4 + i:5 + i]), ["xt%d" % i], ["sqj", "statp%d" % i])
                for i in range(4):
                    S.add(ACT, lambda e, i=i: e.activation(out=stat[:, 4 + i:5 + i], in_=stat[:, 4 + i:5 + i], func=AF.Ln,
                                                          scale=1.0 / D, bias=epst[:]), ["statp%d" % i, "epst"], ["statp%d" % i])
                for i in range(4):
                    S.add(ACT, lambda e, i=i: e.activation(out=stat[:, 4 + i:5 + i], in_=stat[:, 4 + i:5 + i], func=AF.Exp, scale=-0.5),
                          ["statp%d" % i], ["statp%d" % i])
                for i in range(4):
                    xn = "xt%d" % i
                    h = hs[i % 2]
                    hn = "hs%d" % (i % 2)
                    S.add(DVE, lambda e, h=h, i=i: e.scalar_tensor_tensor(out=h[:], in0=xt[:, i, :], scalar=stat[:, 4 + i:5 + i],
                                                                            in1=gpre[:], op0=ALU.mult, op1=ALU.mult),
                          [xn, "statp%d" % i, "gpre"], [hn])
                    for kc in range(8):
                        S.add(PE, lambda e, h=h, kc=kc: e.transpose(out=bankT[:, kc * 128:(kc + 1) * 128],
                                                                     in_=h[:, kc * 128:(kc + 1) * 128], identity=identb[:]),
                              [hn, "identb"], [BT])
                    S.add(ACT, lambda e, i=i: e.activation(out=hT[:, :, i * 128:(i + 1) * 128],
                                                          in_=bankT[:].rearrange("p (k t) -> p k t", k=8), func=AF.Copy),
                          [BT], ["hT"])

                def proj_group(col0, evac):
                    i = load_win(l, col0)
                    wn = "win%d" % i
                    for ch in range(4):
                        b = ch
                        for kc in range(8):
                            S.add(PE, lambda e, i=i, ch=ch, kc=kc, b=b: e.matmul(
                                bank[b][:], lhsT=win[i][:, kc, ch * 128:(ch + 1) * 128], rhs=hT[:, kc, :],
                                start=(kc == 0), stop=(kc == 7), skip_group_check=True),
                                ["hT", wn], [BN[b]])
                        evac(ch, b)
                    return i

                def proj_tok(i, name, dst_dram, keep_bf=None):
                    wn = "win%d" % i
                    for sub in range(4):
                        b = sub % 2
                        for kc in range(8):
                            S.add(PE, lambda e, sub=sub, kc=kc, b=b: e.matmul(
                                bank[b][:], lhsT=hT[:, kc, sub * 128:(sub + 1) * 128], rhs=win[i][:, kc, :],
                                start=(kc == 0), stop=(kc == 7), skip_group_check=True),
                                ["hT", wn], [BN[b]])
                        kv = kvst[b]
                        S.add(ACT, lambda e, kv=kv, b=b: e.activation(out=kv[:], in_=bank[b][:], func=AF.Copy),
                              [BN[b]], ["kvst%d" % b])
                        S.add(SP, lambda e, kv=kv, sub=sub: e.dma_start(
                            out=dst_dram[l, t0 + sub * 128:t0 + (sub + 1) * 128, :], in_=kv[:]),
                            ["kvst%d" % b], dma=True, chan="st")
                        if keep_bf is not None:
                            S.add(ACT, lambda e, kv=kv, sub=sub: e.activation(out=keep_bf[:, 4 * q + sub, :], in_=kv[:], func=AF.Copy),
                                  ["kvst%d" % b], ["v_all"])

                def br_c():
                    if q == 0:
                        pool_op(lambda e: e.memset(u_ext[:, :, 0:30], 0.0), ["u_ext"], ["u_ext"])
                    else:
                        for ch in range(4):
                            pool_op(lambda e, ch=ch: e.tensor_copy(out=u_ext[:, ch, 0:30], in_=u_ext[:, ch, 512:542]),
                                    ["u_ext"], ["u_ext"])
                    proj_group(512, lambda ch, b: S.add(
                        ACT, lambda e: e.activation(out=u_ext[:, ch, 30:542], in_=bank[b][:], func=AF.Sigmoid),
                        [BN[b]], ["u_ext"]))
                    proj_group(0, lambda ch, b: S.add(
                        DVE, lambda e: e.tensor_tensor(out=u_ext[:, ch, 30:542], in0=bank[b][:], in1=u_ext[:, ch, 30:542], op=ALU.mult),
                        [BN[b], "u_ext"], ["u_ext"]))
                    proj_group(1024, lambda ch, b: S.add(
                        ACT, lambda e: e.activation(out=mixT[:, ch, :], in_=bank[b][:], func=AF.Silu), [BN[b]], ["mixT%d" % ch]))
                    for k in range(0, 31):
                        for ch in range(4):
                            yn = "yconv%d" % ch
                            if k == 0:
                                S.add(DVE, lambda e, ch=ch: e.tensor_scalar(out=yconv[:, ch, :], in0=u_ext[:, ch, 0:512],
                                                                           scalar1=ctab[:, ch, 0:1], scalar2=ctab[:, ch, 31:32],
                                                                           op0=ALU.mult, op1=ALU.add), ["u_ext", "ctab"], [yn])
                            else:
                                S.add(DVE, lambda e, ch=ch, k=k: e.scalar_tensor_tensor(
                                    out=yconv[:, ch, :], in0=u_ext[:, ch, k:k + 512], scalar=ctab[:, ch, k:k + 1],
                                    in1=yconv[:, ch, :], op0=ALU.mult, op1=ALU.add), ["u_ext", "ctab", yn], [yn])
                def br_c_tail():
                    for ch in range(4):
                        yn = "yconv%d" % ch
                        S.add(ACT, lambda e, ch=ch: e.activation(out=aTb[0][:], in_=yconv[:, ch, :], func=AF.Copy),
                              [yn], ["aTb0"])
                        S.add(ACT, lambda e, ch=ch: e.activation(out=aTb[1][:], in_=yconv[:, ch, :], func=AF.Square),
                              [yn], ["aTb1"])
                        S.add(PE, lambda e, ch=ch: e.matmul(bank[0][:], lhsT=inv512[:], rhs=aTb[0][:], start=(ch == 0),
                                                            stop=(ch == 3), skip_group_check=True), ["aTb0", "inv512"], [BN[0]])
                        S.add(PE, lambda e, ch=ch: e.matmul(bank[1][:], lhsT=inv512[:], rhs=aTb[1][:], start=(ch == 0),
                                                            stop=(ch == 3), skip_group_check=True), ["aTb1", "inv512"], [BN[1]])
                    S.add(ACT, lambda e: e.activation(out=stats[:, 0, :], in_=bank[0][:], func=AF.Copy), [BN[0]], ["stats0"])
                    S.add(ACT, lambda e: e.activation(out=stats[:, 1, :], in_=bank[0][:], func=AF.Square), [BN[0]], ["stats1"])
                    S.add(DVE, lambda e: e.tensor_tensor(out=stats[:, 1, :], in0=bank[1][:], in1=stats[:, 1, :], op=ALU.subtract),
                          [BN[1], "stats1"], ["stats1"])
                    S.add(ACT, lambda e: e.activation(out=stats[:, 1, :], in_=stats[:, 1, :], func=AF.Ln, bias=epst[:]),
                          ["stats1", "epst"], ["stats1"])
                    S.add(ACT, lambda e: e.activation(out=stats[:, 1, :], in_=stats[:, 1, :], func=AF.Exp, scale=-0.5),
                          ["stats1"], ["stats1"])
                    for ch in range(4):
                        yn = "yconv%d" % ch
                        eng = DVE
                        S.add(eng, lambda e, ch=ch: e.tensor_tensor(out=yconv[:, ch, :], in0=yconv[:, ch, :], in1=stats[:, 0, :],
                                                                   op=ALU.subtract), [yn, "stats0"], [yn])
                        S.add(eng, lambda e, ch=ch: e.tensor_tensor(out=yconv[:, ch, :], in0=yconv[:, ch, :], in1=stats[:, 1, :],
                                                                   op=ALU.mult), [yn, "stats1"], [yn])
                        S.add(ACT, lambda e, ch=ch: e.activation(out=sconv[:, ch, :], in_=yconv[:, ch, :], func=AF.Silu,
                                                                scale=ctab[:, ch, 32:33], bias=ctab[:, ch, 33:34]),
                              [yn, "ctab"], ["sconv"])
                    for co in range(4):
                        b = co % 2
                        for ch in range(4):
                            S.add(PE, lambda e, co=co, ch=ch, b=b: e.matmul(
                                bank[b][:], lhsT=wpw[:, ch, co * 128:(co + 1) * 128], rhs=sconv[:, ch, :],
                                start=(ch == 0), stop=(ch == 3), skip_group_check=True), ["sconv", "wpw"], [BN[b]])
                        S.add(DVE, lambda e, co=co, b=b: e.scalar_tensor_tensor(
                            out=mixT[:, co, :], in0=bank[b][:], scalar=ctab[:, co, 34:35], in1=mixT[:, co, :],
                            op0=ALU.add, op1=ALU.mult), [BN[b], "ctab", "mixT%d" % co], ["mixT%d" % co])
                    if last:
                        for ch in range(4):
                            S.add(PE, lambda e, ch=ch: e.transpose(out=bank[0][0:30, ch * 128:(ch + 1) * 128],
                                                                   in_=u_ext[:, ch, 512:542], identity=identf[:]),
                                  ["u_ext", "identf"], [BN[0]])
                        S.add(ACT, lambda e: e.activation(out=outst[0:30, :], in_=bank[0][0:30, :], func=AF.Copy), [BN[0]], ["kvst0"])
                        S.add(SP, lambda e: e.dma_start(out=conv_p[l], in_=outst[0:30, :]), ["kvst0"], dma=True, chan="st")

                def br_p():
                    if q == 0:
                        pool_op(lambda e: e.memset(pu_ext[:, :, 0:15], 0.0), ["pu_ext"], ["pu_ext"])
                    else:
                        for ch in range(4):
                            pool_op(lambda e, ch=ch: e.tensor_copy(out=pu_ext[:, ch, 0:15], in_=pu_ext[:, ch, 512:527]),
                                    ["pu_ext"], ["pu_ext"])
                    proj_group(1536, lambda ch, b: S.add(
                        ACT, lambda e: e.activation(out=pu_ext[:, ch, 15:527], in_=bank[b][:], func=AF.Copy), [BN[b]], ["pu_ext"]))
                    proj_group(2048, lambda ch, b: S.add(
                        ACT, lambda e: e.activation(out=mixT[:, 4 + ch, :], in_=bank[b][:], func=AF.Silu), [BN[b]], ["mixT%d" % (4 + ch)]))
                def br_p_tail():
                    for g, w in enumerate(POOLW):
                        cur = pu_ext[:, g, :]
                        curn = ("pu_ext",)
                        lo = 0
                        d = 1
                        k = 0
                        while d < w:
                            nxt = ptmp[k % 2]
                            nn = ("yconv0", "yconv1") if k % 2 == 0 else ("yconv2", "yconv3")
                            S.add(DVE, lambda e, cur=cur, nxt=nxt, lo=lo, d=d: e.tensor_tensor(
                                out=nxt[:, lo + d:527], in0=cur[:, lo + d:527], in1=cur[:, lo:527 - d], op=ALU.add),
                                list(curn), list(nn))
                            cur, curn = nxt, nn
                            lo += d
                            d *= 2
                            k += 1
                        S.add(DVE, lambda e, cur=cur, g=g, w=w: e.scalar_tensor_tensor(
                            out=dT[:, g, :], in0=cur[:, 15:527], scalar=1.0 / w, in1=pu_ext[:, g, 15:527],
                            op0=ALU.mult, op1=ALU.subtract), list(curn) + ["pu_ext"], ["sconv"])
                        if q == 0:
                            S.add(DVE, lambda e, cur=cur, g=g: e.tensor_tensor(
                                out=cur[:, 15:31], in0=cur[:, 15:31], in1=corr[:, g, :], op=ALU.mult), list(curn) + ["corr", "sconv"], list(curn))
                            S.add(DVE, lambda e, cur=cur, g=g: e.tensor_tensor(
                                out=dT[:, g, 0:16], in0=cur[:, 15:31], in1=pu_ext[:, g, 15:31], op=ALU.subtract),
                                list(curn) + ["pu_ext"], ["sconv"])
                        b = g % 2
                        S.add(PE, lambda e, g=g, b=b: e.matmul(bank[b][:], lhsT=wpool[:, g, :], rhs=dT[:, g, :], start=True, stop=True,
                                                               skip_group_check=True), ["sconv", "wpool"], [BN[b]])
                        S.add(DVE, lambda e, g=g, b=b: e.scalar_tensor_tensor(
                            out=mixT[:, 4 + g, :], in0=bank[b][:], scalar=ctab[:, g, 35:36], in1=mixT[:, 4 + g, :],
                            op0=ALU.mult, op1=ALU.mult), [BN[b], "ctab", "mixT%d" % (4 + g)], ["mixT%d" % (4 + g)])
                    if last:
                        for ch in range(4):
                            S.add(PE, lambda e, ch=ch: e.transpose(out=bank[0][0:15, ch * 128:(ch + 1) * 128],
                                                                   in_=pu_ext[:, ch, 512:527], identity=identf[:]),
                                  ["pu_ext", "identf"], [BN[0]])
                        S.add(ACT, lambda e: e.activation(out=outst[0:15, :], in_=bank[0][0:15, :], func=AF.Copy), [BN[0]], ["kvst0"])
                        S.add(SP, lambda e: e.dma_start(out=pool_p[l], in_=outst[0:15, :]), ["kvst0"], dma=True, chan="st")

                def br_s():
                    proj_group(2560, lambda ch, b: S.add(
                        ACT, lambda e: e.activation(out=qT[:, ch, :], in_=bank[b][:], func=AF.Copy, scale=0.125), [BN[b]], ["qT"]))
                    iw = proj_group(3072, lambda ch, b: S.add(
                        ACT, lambda e: e.activation(out=kT_all[:, ch, t0:t0 + 512], in_=bank[b][:], func=AF.Copy), [BN[b]], ["kT_all"]))
                    proj_tok(iw, "k", k_p)
                    iw = load_win(l, 3584)
                    proj_tok(iw, "v", v_p, keep_bf=v_all)
                    proj_group(4096, lambda ch, b: S.add(
                        ACT, lambda e: e.activation(out=gate[:, ch, :], in_=bank[b][:], func=AF.Silu), [BN[b]], ["gate"]))
                def br_s_attn():
                    S.add(POOL, lambda e: e.dma_start(out=wout[:], in_=w_out[l][:, 0:512].rearrange("(q p) f -> p q f", p=128)),
                          writes=["wout"], dma=True, chan="w")
                    steps = [(h, kb) for h in range(8) for kb in range(4 * q + 3, -1, -1)]

                    def geom(n):
                        h, kb = steps[n]
                        hp, hl = h // 2, h % 2
                        pr = slice(hl * 64, hl * 64 + 64)
                        dg = kb - 4 * q
                        c0 = 128 * dg if dg > 0 else 0
                        return h, kb, hp, pr, c0, (dg >= 0), n % 2

                    def stage1(n):
                        h, kb, hp, pr, c0, diag, nb = geom(n)
                        A, An = bank[2 + nb], BN[2 + nb]
                        e_t, e_n = esb[nb], "esb%d" % nb
                        sp_t, sp_n = spb[nb], "spb%d" % nb
                        ksl = slice(kb * 128, kb * 128 + 128)
                        S.add(PE, lambda e: e.matmul(A[:, c0:512], lhsT=kT_all[pr, hp, ksl], rhs=qT[pr, hp, c0:512], start=True, stop=True,
                                                     skip_group_check=True), ["kT_all", "qT"], [An])
                        S.add(ACT, lambda e: e.activation(out=e_t[:, c0:512], in_=A[:, c0:512], func=AF.Exp, bias=sbb_t[:, h:h + 1]),
                              [An, "sbb_t"], [e_n])

                    def stage1b(n):
                        h, kb, hp, pr, c0, diag, nb = geom(n)
                        e_t, e_n = esb[nb], "esb%d" % nb
                        sp_t, sp_n = spb[nb], "spb%d" % nb
                        S.add(ACT, lambda e: e.activation(out=sp_t[:, c0:512], in_=e_t[:, c0:512], func=AF.Ln, bias=1.0), [e_n], [sp_n])
                        if diag:
                            pool_op(lambda e: e.tensor_tensor(out=sp_t[:, c0:c0 + 128], in0=sp_t[:, c0:c0 + 128], in1=mask01[:], op=ALU.mult),
                                    [sp_n, "mask01"], [sp_n])

                    def stage2(n):
                        h, kb, hp, pr, c0, diag, nb = geom(n)
                        first = (kb == 4 * q + 3)
                        Bk, Bn = bank[4 + nb], BN[4 + nb]
                        sp_t, sp_n = spb[nb], "spb%d" % nb
                        a_t, a_n = aTb[nb], "aTb%d" % nb
                        ob = 6 if h % 2 == 0 else 1
                        ksl = slice(kb * 128, kb * 128 + 128)
                        if first:
                            pool_op(lambda e: e.memset(spacc[:], 0.0), ["spacc"], ["spacc"])
                            pool_op(lambda e: e.memset(spaccb[:], 0.0), ["spaccb"], ["spaccb"])
                        S.add(PE, lambda e: e.matmul(Bk[:, c0:512], lhsT=negU[:], rhs=sp_t[:, c0:512], start=True, stop=False,
                                                     skip_group_check=True), [sp_n, "negU"], [Bn])
                        if not first:
                            S.add(PE, lambda e: e.matmul(Bk[:, c0:512], lhsT=negOnes[:], rhs=spaccb[:, c0:512], start=False, stop=False,
                                                         skip_group_check=True), ["spaccb", "negOnes"], [Bn])
                        S.add(PE, lambda e: e.matmul(Bk[:, c0:512], lhsT=kT_all[pr, hp, ksl], rhs=qT[pr, hp, c0:512], start=False, stop=True,
                                                     skip_group_check=True), ["kT_all", "qT"], [Bn])
                        S.add(ACT, lambda e: e.activation(out=a_t[:, c0:512], in_=Bk[:, c0:512], func=AF.Exp, bias=sbb_t[:, h:h + 1]),
                              [Bn, "sbb_t"], [a_n])
                        if diag:
                            pool_op(lambda e: e.tensor_tensor(out=a_t[:, c0:c0 + 128], in0=a_t[:, c0:c0 + 128], in1=mask01[:], op=ALU.mult),
                                    [a_n, "mask01"], [a_n])
                        S.add(PE, lambda e: e.matmul(bank[ob][pr, c0:512], lhsT=v_all[:, kb, h * 64:(h + 1) * 64], rhs=a_t[:, c0:512],
                                                     start=first, stop=(kb == 0), skip_group_check=True), [a_n, "v_all"], [BN[ob]])
                        if kb > 0:
                            pool_op(lambda e: e.tensor_tensor(out=spacc[:, c0:512], in0=spacc[:, c0:512], in1=sp_t[:, c0:512], op=ALU.add),
                                    ["spacc", sp_n], ["spacc"])
                            S.add(DVE, lambda e: e.tensor_copy(out=spaccb[:, c0:512], in_=spacc[:, c0:512]), ["spacc", "spaccb"], ["spaccb"])
                        else:
                            S.add(DVE, lambda e: e.tensor_tensor(out=mixT[pr, 8 + hp, :], in0=bank[ob][pr, :], in1=gate[pr, hp, :], op=ALU.mult),
                                  [BN[ob], "gate"], ["mixT%d" % (8 + hp)])

                    stage1(0)
                    stage1b(0)
                    for n in range(len(steps)):
                        if n + 1 < len(steps):
                            stage1(n + 1)
                        stage2(n)
                        if n + 1 < len(steps):
                            stage1b(n + 1)

                def br_m():
                    proj_group(4608, lambda ch, b: S.add(
                        ACT, lambda e: e.activation(out=qT[:, ch, :], in_=bank[b][:], func=AF.Copy, scale=128 ** -0.5),
                        [BN[b]], ["qT"]))
                    proj_group(5120, lambda ch, b: S.add(
                        ACT, lambda e: e.activation(out=gate[:, ch, :], in_=bank[b][:], func=AF.Silu), [BN[b]], ["gate"]))
                    for hh in range(4):
                        for mc in range(2):
                            b = 2 + mc
                            S.add(PE, lambda e, hh=hh, mc=mc, b=b: e.matmul(
                                bank[b][:], lhsT=mkT[:, hh, mc * 128:(mc + 1) * 128], rhs=qT[:, hh, :], start=True, stop=True,
                                skip_group_check=True), ["mkT", "qT"], [BN[b]])
                            S.add(ACT, lambda e, mc=mc, b=b: e.activation(out=spb[mc][:], in_=bank[b][:], func=AF.Exp),
                                  [BN[b]], ["spb%d" % mc])
                        for mc in range(2):
                            S.add(PE, lambda e, hh=hh, mc=mc: e.matmul(
                                bank[4][:], lhsT=mvb[:, mc, hh * 128:(hh + 1) * 128], rhs=spb[mc][:], start=(mc == 0),
                                stop=(mc == 1), skip_group_check=True), ["mvb", "spb%d" % mc], [BN[4]])
                            S.add(PE, lambda e, mc=mc: e.matmul(
                                bank[5][:], lhsT=onesb[:], rhs=spb[mc][:], start=(mc == 0), stop=(mc == 1),
                                skip_group_check=True), ["onesb", "spb%d" % mc], [BN[5]])
                        S.add(DVE, lambda e: e.reciprocal(out=rden[:], in_=bank[5][:]), [BN[5]], ["esb0"])
                        S.add(DVE, lambda e, hh=hh: e.tensor_tensor(out=gtmp[:], in0=rden[:], in1=gate[:, hh, :], op=ALU.mult),
                              ["esb0", "gate"], ["esb1"])
                        S.add(DVE, lambda e, hh=hh: e.tensor_tensor(out=mixT[:, 12 + hh, :], in0=bank[4][:], in1=gtmp[:], op=ALU.mult),
                              [BN[4], "esb1"], ["mixT%d" % (12 + hh)])

                br_c()
                br_p()
                br_s()
                br_c_tail()
                br_p_tail()
                br_s_attn()
                br_m()
                if DEBUG and l == 0 and q == DEBUG_Q:
                    for f in range(16):
                        S.add(ACT, lambda e, f=f: e.activation(out=esb[f % 2][:], in_=mixT[:, f, :], func=AF.Copy),
                              ["mixT%d" % f], ["esb%d" % (f % 2)])
                        S.add(SP, lambda e, f=f: e.dma_start(out=dbg[:, f, :], in_=esb[f % 2][:]), ["esb%d" % (f % 2)],
                              dma=True, chan="st")
                for half in range(2):
                    if half == 1:
                        S.add(POOL, lambda e, half=half: e.dma_start(
                            out=wout[:], in_=w_out[l][:, half * 512:(half + 1) * 512].rearrange("(q p) f -> p q f", p=128)),
                            writes=["wout"], dma=True, chan="w")
                    for sub in range(4):
                        b = 2 * half + (sub % 2) if False else None
                    for sub in range(4):
                        bi = sub
                        pass
                    for sub in range(4):
                        b = (sub % 2) + 2 * half
                        for f in range(16):
                            S.add(PE, lambda e, sub=sub, f=f, b=b: e.matmul(
                                bank[b][:], lhsT=mixT[:, f, sub * 128:(sub + 1) * 128], rhs=wout[:, f, :],
                                start=(f == 0), stop=(f == 15), skip_group_check=True), ["mixT%d" % f, "wout"], [BN[b]])
                        S.add(ACT, lambda e, sub=sub, half=half, b=b: e.activation(
                            out=ostash[:, sub, half * 512:(half + 1) * 512], in_=bank[b][:], func=AF.Copy),
                            [BN[b]], ["ostash%d" % sub])
                for sub in range(4):
                    ti = 4 * q + sub
                    on = "ostash%d" % sub
                    S.add(ACT, lambda e, sub=sub: e.activation(out=sqj[:], in_=ostash[:, sub, :], func=AF.Square,
                                                              accum_out=stat[:, 2:3]), [on], ["sqj", "stat"])
                    S.add(ACT, lambda e: e.activation(out=stat[:, 2:3], in_=stat[:, 2:3], func=AF.Ln, scale=1.0 / D,
                                                      bias=epst[:]), ["stat", "epst"], ["stat"])
                    S.add(ACT, lambda e: e.activation(out=stat[:, 2:3], in_=stat[:, 2:3], func=AF.Exp, scale=-0.5),
                          ["stat"], ["stat"])
                    S.add(DVE, lambda e, sub=sub: e.scalar_tensor_tensor(
                        out=ostash[:, sub, :], in0=ostash[:, sub, :], scalar=stat[:, 2:3], in1=gpost[:],
                        op0=ALU.mult, op1=ALU.mult), [on, "stat", "gpost"], [on])
                    S.add(DVE, lambda e, sub=sub: e.tensor_tensor(
                        out=ostash[:, sub, :], in0=xt[:, sub, :], in1=ostash[:, sub, :], op=ALU.add), [on, "xt%d" % sub], [on])
                    if l < NL - 1:
                        S.add(SP, lambda e, sub=sub, ti=ti: e.dma_start(out=x1[ti * 128:(ti + 1) * 128, :], in_=ostash[:, sub, :]),
                              [on], ["x1_%d" % ti], dma=True, chan="st")
                    else:
                        S.add(ACT, lambda e, sub=sub: e.activation(out=sqj[:], in_=ostash[:, sub, :], func=AF.Square,
                                                                  accum_out=stat[:, 3:4]), [on], ["sqj", "stat"])
                        S.add(ACT, lambda e: e.activation(out=stat[:, 3:4], in_=stat[:, 3:4], func=AF.Ln, scale=1.0 / D,
                                                          bias=epst[:]), ["stat", "epst"], ["stat"])
                        S.add(ACT, lambda e: e.activation(out=stat[:, 3:4], in_=stat[:, 3:4], func=AF.Exp, scale=-0.5),
                              ["stat"], ["stat"])
                        S.add(DVE, lambda e, sub=sub: e.scalar_tensor_tensor(
                            out=ostash[:, sub, :], in0=ostash[:, sub, :], scalar=stat[:, 3:4], in1=gmisc[:],
                            op0=ALU.mult, op1=ALU.mult), [on, "stat", "gmisc"], [on])
                        S.add(SP, lambda e, sub=sub, ti=ti: e.dma_start(out=y_p[ti * 128:(ti + 1) * 128, :], in_=ostash[:, sub, :]),
                              [on], dma=True, chan="st")

            ostash = sb("ostash", [128, 4, D])

            for l in range(NL):
                load_layer_consts(l)
                mem_kv_layer(l)
                if l == NL - 1:
                    S.add(SP, lambda e: e.dma_start(out=gmisc[:], in_=final_g[0:1, :].partition_broadcast(128)),
                          writes=["gmisc"], dma=True, chan="ld")
                for q in range(4):
                    prompt_tile(l, q)
        S.emit()
    return nc


def prep_core(inp, c, NP, NPHYS, ckv=None):
    f = np.ascontiguousarray
    if ckv is None:
        ckv = np.concatenate([inp["cache_k"].reshape(DEPTH * NPHYS * 128, 512),
                              inp["cache_v"].reshape(DEPTH * NPHYS * 128, 512)], axis=1)
    cvec = np.concatenate([inp["conv_w_dw"], inp["conv_b_dw"][:, None, :], inp["conv_ln_g"][:, None, :],
                           inp["conv_ln_b"][:, None, :], inp["conv_b_pw"][:, None, :],
                           inp["pool_scale"][:, None, :]], axis=1)
    sl = slice(NSEQ * c, NSEQ * c + NSEQ)
    return {
        "xp": f(inp["x_prompt"][c]),
        "xs": f(inp["x_sample"][sl].reshape(NTOK, D)),
        "memp": f(inp["mem_prompt"][c]),
        "ckv": ckv,
        "cmk": f(inp["cache_mem_k"][:, sl].reshape(DEPTH, NSEQ, NMEM, 512)),
        "cmv": f(inp["cache_mem_v"][:, sl].reshape(DEPTH, NSEQ, NMEM, 512)),
        "stc": f(inp["state_conv"][:, sl]),
        "stp": f(inp["state_pool"][:, sl]),
        "ptab": f(inp["page_table"][sl].reshape(1, NSEQ * NP).astype(np.int32)),
        "pre_g": f(inp["pre_g"]), "post_g": f(inp["post_g"]), "mem_g": f(inp["mem_g"]),
        "final_g": f(inp["final_g"].reshape(1, D)),
        "w_in": inp["w_in"], "w_mem": inp["w_mem_kv"], "w_out": inp["w_out"],
        "w_pw": inp["conv_w_pw"], "pool_w": inp["pool_w"], "cvec": f(cvec), "sbb": f(inp["sb_bias"]),
    }


def assemble(results, ncores):
    r = results
    cat = lambda name: np.stack([r[c][name] for c in range(ncores)])
    y_p = cat("y_p")
    y_s = cat("y_s").reshape(ncores * NSEQ, 4, D)
    k_p = cat("k_p").transpose(1, 0, 2, 3).reshape(DEPTH, ncores, SEQ, 8, 64)
    v_p = cat("v_p").transpose(1, 0, 2, 3).reshape(DEPTH, ncores, SEQ, 8, 64)
    k_s = cat("k_s").transpose(1, 0, 2, 3).reshape(DEPTH, ncores * NSEQ, 4, 8, 64)
    v_s = cat("v_s").transpose(1, 0, 2, 3).reshape(DEPTH, ncores * NSEQ, 4, 8, 64)
    conv_p = cat("conv_p").transpose(1, 0, 2, 3)
    conv_s = cat("conv_s").transpose(1, 0, 2, 3, 4).reshape(DEPTH, ncores * NSEQ, 30, 512)
    pool_p = cat("pool_p").transpose(1, 0, 2, 3)
    pool_s = cat("pool_s").transpose(1, 0, 2, 3, 4).reshape(DEPTH, ncores * NSEQ, 15, 512)
    mk_p = cat("mk_p").transpose(1, 0, 2, 3).reshape(DEPTH, ncores, NMEM, 4, 128)
    mv_p = cat("mv_p").transpose(1, 0, 2, 3).reshape(DEPTH, ncores, NMEM, 4, 128)
    return tuple(np.ascontiguousarray(a, dtype=np.float32) for a in
                 (y_p, y_s, k_p, v_p, k_s, v_s, conv_p, conv_s, pool_p, pool_s, mk_p, mv_p))


def kernel(**inputs):
    inp = {k: np.asarray(v) for k, v in inputs.items()}
    NP = inp["page_table"].shape[1]
    NPHYS = inp["cache_k"].shape[1]
    nc = build(NP=NP, NPHYS=NPHYS)
    ckv = np.concatenate([inp["cache_k"].reshape(DEPTH * NPHYS * 128, 512),
                          inp["cache_v"].reshape(DEPTH * NPHYS * 128, 512)], axis=1)
    in_maps = [prep_core(inp, c, NP, NPHYS, ckv) for c in range(NCORE)]
    res = run_bass_kernel_spmd(nc, in_maps, core_ids=list(range(NCORE)))
    return assemble(res.results, NCORE)
```
